# Optimizing a Trainium2 kernel written in Bass

```python
import math
import jax, jax.numpy as jnp
from jax import lax
import numpy as np

D_MODEL = 1024
BATCH = 8
SEQ = 2048
DEPTH = 4

N_MEM = 256
EPS = 1e-6
NEG_INF = -1e30
Q_BLOCK = 128

DIFF_HEADS = 8
DIFF_QK_DIM = 64
DIFF_V_DIM = 2 * DIFF_QK_DIM
DIFF_QK_WIDTH = 2 * DIFF_HEADS * DIFF_QK_DIM
DIFF_WIDTH = DIFF_HEADS * DIFF_V_DIM

DIL_GROUPS = ((128, 1), (512, 4), (2048, 16))
N_DIL_GROUPS = len(DIL_GROUPS)
DIL_HEADS = 4
DIL_HEAD_DIM = 128
DIL_WIDTH = DIL_HEADS * DIL_HEAD_DIM
DIL_QKV_WIDTH = N_DIL_GROUPS * DIL_WIDTH

MEM_HEADS = 4
MEM_HEAD_DIM = 128
MEM_WIDTH = MEM_HEADS * MEM_HEAD_DIM

N_BRANCH = 3

REL_BUCKETS = 32
REL_MAX_DIST = 1024
N_BIAS_HEADS = DIFF_HEADS + N_DIL_GROUPS * DIL_HEADS

IN_SIZES = (DIFF_QK_WIDTH, DIFF_QK_WIDTH, DIFF_WIDTH, DIFF_WIDTH,
            DIL_QKV_WIDTH, DIL_QKV_WIDTH, DIL_QKV_WIDTH, DIL_WIDTH,
            MEM_WIDTH, MEM_WIDTH, N_BRANCH * D_MODEL)
N_IN = sum(IN_SIZES)

kernel_name = 'hybrid_gated_diff_dilated_mem_encoder'


def rmsnorm(x, g=None):
    xf = x.astype(jnp.float32)
    y = xf * lax.rsqrt(jnp.mean(xf * xf, axis=-1, keepdims=True) + EPS)
    if g is not None:
        y = y * g.astype(jnp.float32)
    return y.astype(x.dtype)


def t5_bucket(rel):
    half = REL_BUCKETS // 2
    max_exact = half // 2
    ret = jnp.where(rel > 0, half, 0)
    n = jnp.abs(rel)
    nf = jnp.maximum(n, 1).astype(jnp.float32)
    large = max_exact + (jnp.log(nf / max_exact) / math.log(REL_MAX_DIST / max_exact)
                         * (half - max_exact)).astype(jnp.int32)
    large = jnp.minimum(large, half - 1)
    return ret + jnp.where(n < max_exact, n, large)


def diff_attention(q, k, v, lam, bias_tab):
    B, S = q.shape[0], q.shape[1]
    nq = S // Q_BLOCK
    scale = DIFF_QK_DIM ** -0.5
    qb = q.reshape(B, nq, Q_BLOCK, 2, DIFF_HEADS, DIFF_QK_DIM).transpose(1, 0, 2, 3, 4, 5)
    kpos = jnp.arange(S, dtype=jnp.int32)

    def one_block(args):
        qblk, i = args
        qpos = i * Q_BLOCK + jnp.arange(Q_BLOCK, dtype=jnp.int32)
        bias = bias_tab[t5_bucket(kpos[None, :] - qpos[:, None])]
        bias = jnp.transpose(bias, (2, 0, 1)).astype(jnp.float32)
        s = jnp.einsum('bqmhd,bkmhd->bmhqk', qblk, k,
                       preferred_element_type=jnp.float32) * scale + bias
        p = jax.nn.softmax(s, axis=-1)
        w = p[:, 0] - lam * p[:, 1]
        return jnp.einsum('bhqk,bkhe->bqhe', w.astype(v.dtype), v)

    out = lax.map(one_block, (qb, jnp.arange(nq, dtype=jnp.int32)))
    return out.transpose(1, 0, 2, 3, 4).reshape(B, S, DIFF_HEADS, DIFF_V_DIM)


def dilated_group(q, k, v, window, dilation, bias_tab):
    B, S, H, dh = q.shape
    r = dilation
    half = window // (2 * r)
    nb = half
    L = S // r
    nblk = -(-L // nb)
    Lp = nblk * nb
    Bn = B * r
    scale = dh ** -0.5

    def to_strided(t):
        return t.reshape(B, L, r, H, dh).transpose(0, 2, 1, 3, 4).reshape(Bn, L, H, dh)

    def band(t):
        tp = jnp.pad(t, ((0, 0), (nb, Lp - L + nb), (0, 0), (0, 0)))
        tp = tp.reshape(Bn, nblk + 2, nb, H, dh)
        return jnp.concatenate([tp[:, :-2], tp[:, 1:-1], tp[:, 2:]], axis=2)

    qs = jnp.pad(to_strided(q), ((0, 0), (0, Lp - L), (0, 0), (0, 0))).reshape(Bn, nblk, nb, H, dh)
    kb = band(to_strided(k))
    vb = band(to_strided(v))

    qi = jnp.arange(nb, dtype=jnp.int32)
    kj = jnp.arange(3 * nb, dtype=jnp.int32) - nb
    delta = kj[None, :] - qi[:, None]
    ksub = jnp.arange(nblk, dtype=jnp.int32)[:, None] * nb + kj[None, :]
    valid = (jnp.abs(delta) <= half)[None, :, :] & ((ksub >= 0) & (ksub < L))[:, None, :]
    bias = bias_tab[t5_bucket(delta * r)].astype(jnp.float32).transpose(2, 0, 1)

    s = jnp.einsum('bnqhd,bnkhd->bnhqk', qs, kb,
                   preferred_element_type=jnp.float32) * scale + bias
    s = jnp.where(valid[None, :, None], s, NEG_INF)
    m = jnp.max(s, axis=-1, keepdims=True)
    e = jnp.exp(s - m)
    den = jnp.sum(e, axis=-1, keepdims=True)
    o = jnp.einsum('bnhqk,bnkhd->bnqhd', (e / den).astype(v.dtype), vb)
    lse = (m + jnp.log(den))[..., 0]

    o = o.reshape(Bn, Lp, H, dh)[:, :L]
    lse = lse.transpose(0, 1, 3, 2).reshape(Bn, Lp, H)[:, :L]
    o = o.reshape(B, r, L, H, dh).transpose(0, 2, 1, 3, 4).reshape(B, S, H, dh)
    lse = lse.reshape(B, r, L, H).transpose(0, 2, 1, 3).reshape(B, S, H)
    return o, lse


def memory_attention(q, mk, mv):
    s = jnp.einsum('bshd,bmhd->bhsm', q, mk,
                   preferred_element_type=jnp.float32) * (MEM_HEAD_DIM ** -0.5)
    p = jax.nn.softmax(s, axis=-1)
    return jnp.einsum('bhsm,bmhd->bshd', p.astype(mv.dtype), mv)


def mixer_layer(x, mem, layer_idx, g_norm, w_in, lam_p, w_mem_kv, g_mem,
                w_br_diff, w_br_dil, w_br_mem, w_out, rel_bias):
    B, S, _ = x.shape
    h = rmsnorm(x, g_norm)
    z = jnp.einsum('bsd,dn->bsn', h, w_in)
    splits = np.cumsum(IN_SIZES)[:-1].tolist()
    dq, dk, dv, dg, lq, lk, lv, lg, mq, mg, mgate = jnp.split(z, splits, axis=-1)

    lam_init = 0.8 - 0.6 * math.exp(-0.3 * layer_idx)
    lp = lam_p.astype(jnp.float32)
    lam = jnp.exp(jnp.dot(lp[0], lp[1])) - jnp.exp(jnp.dot(lp[2], lp[3])) + lam_init
    o_a = diff_attention(dq.reshape(B, S, 2, DIFF_HEADS, DIFF_QK_DIM),
                         dk.reshape(B, S, 2, DIFF_HEADS, DIFF_QK_DIM),
                         dv.reshape(B, S, DIFF_HEADS, DIFF_V_DIM),
                         lam, rel_bias[:, :DIFF_HEADS])
    o_a = rmsnorm(o_a) * (1.0 - lam_init)
    y_a = jnp.einsum('bse,ed->bsd', o_a.reshape(B, S, DIFF_WIDTH) * jax.nn.silu(dg), w_br_diff)

    lq = lq.reshape(B, S, N_DIL_GROUPS, DIL_HEADS, DIL_HEAD_DIM)
    lk = lk.reshape(B, S, N_DIL_GROUPS, DIL_HEADS, DIL_HEAD_DIM)
    lv = lv.reshape(B, S, N_DIL_GROUPS, DIL_HEADS, DIL_HEAD_DIM)
    outs, lses = [], []
    for g, (window, dilation) in enumerate(DIL_GROUPS):
        c0 = DIFF_HEADS + g * DIL_HEADS
        o_g, lse_g = dilated_group(lq[:, :, g], lk[:, :, g], lv[:, :, g], window, dilation,
                                   rel_bias[:, c0:c0 + DIL_HEADS])
        outs.append(o_g.astype(jnp.float32))
        lses.append(lse_g)
    wts = jax.nn.softmax(jnp.stack(lses, axis=0), axis=0)
    o_b = jnp.einsum('gbsh,gbshe->bshe', wts, jnp.stack(outs, axis=0)).astype(x.dtype)
    y_b = jnp.einsum('bse,ed->bsd', o_b.reshape(B, S, DIL_WIDTH) * jax.nn.silu(lg), w_br_dil)

    mem_n = rmsnorm(mem, g_mem)
    kv = jnp.einsum('bmd,dn->bmn', mem_n, w_mem_kv).reshape(B, mem.shape[1], 2, MEM_HEADS, MEM_HEAD_DIM)
    o_m = memory_attention(mq.reshape(B, S, MEM_HEADS, MEM_HEAD_DIM), kv[:, :, 0], kv[:, :, 1])
    y_m = jnp.einsum('bse,ed->bsd', o_m.reshape(B, S, MEM_WIDTH) * jax.nn.silu(mg), w_br_mem)

    gates = jax.nn.sigmoid(mgate).reshape(B, S, N_BRANCH, D_MODEL)
    merged = gates[:, :, 0] * y_a + gates[:, :, 1] * y_b + gates[:, :, 2] * y_m
    return x + jnp.einsum('bsd,de->bse', merged, w_out)


def setup_inputs(seed: int = 0) -> dict:
    key = jax.random.key(seed)
    ks = jax.random.split(key, 14)
    f32 = jnp.float32
    nrm = jax.random.normal
    return {
        'x': nrm(ks[0], (BATCH, SEQ, D_MODEL), f32),
        'mem': nrm(ks[1], (BATCH, N_MEM, D_MODEL), f32),
        'g_norm': 1.0 + 0.02 * nrm(ks[2], (DEPTH, D_MODEL), f32),
        'w_in': nrm(ks[3], (DEPTH, D_MODEL, N_IN), f32) * D_MODEL ** -0.5,
        'diff_lambda': 0.1 * nrm(ks[4], (DEPTH, 4, DIFF_QK_DIM), f32),
        'w_mem_kv': nrm(ks[5], (DEPTH, D_MODEL, 2 * MEM_WIDTH), f32) * D_MODEL ** -0.5,
        'g_mem': 1.0 + 0.02 * nrm(ks[6], (DEPTH, D_MODEL), f32),
        'w_br_diff': nrm(ks[7], (DEPTH, DIFF_WIDTH, D_MODEL), f32) * DIFF_WIDTH ** -0.5,
        'w_br_dil': nrm(ks[8], (DEPTH, DIL_WIDTH, D_MODEL), f32) * DIL_WIDTH ** -0.5,
        'w_br_mem': nrm(ks[9], (DEPTH, MEM_WIDTH, D_MODEL), f32) * MEM_WIDTH ** -0.5,
        'w_out': nrm(ks[10], (DEPTH, D_MODEL, D_MODEL), f32) * D_MODEL ** -0.5,
        'rel_bias': 0.2 * nrm(ks[11], (REL_BUCKETS, N_BIAS_HEADS), f32),
        'g_final': 1.0 + 0.02 * nrm(ks[12], (D_MODEL,), f32),
    }


def reference(x, mem, g_norm, w_in, diff_lambda, w_mem_kv, g_mem,
              w_br_diff, w_br_dil, w_br_mem, w_out, rel_bias, g_final):
    for l in range(DEPTH):
        x = mixer_layer(x, mem, l, g_norm[l], w_in[l], diff_lambda[l], w_mem_kv[l], g_mem[l],
                        w_br_diff[l], w_br_dil[l], w_br_mem[l], w_out[l], rel_bias)
    return rmsnorm(x, g_final)
```

```python
import math
import numpy as np
from contextlib import ExitStack
import concourse.bass as bass
import concourse.mybir as mybir
from concourse.bass_utils import run_bass_kernel_spmd

F32 = mybir.dt.float32
BF16 = mybir.dt.bfloat16
ALU = mybir.AluOpType
AF = mybir.ActivationFunctionType
AX = mybir.AxisListType

NL = 4
S_LEN = 2048
NMEM = 256
DIL = ((128, 1), (512, 4), (2048, 16))
NU = 40
UNIT = 4096
EPS = 1e-6
MASKNEG = -30000.0


class Res:
    __slots__ = ("name", "writer", "readers")

    def __init__(self, name):
        self.name = name
        self.writer = None
        self.readers = []


class Sched:
    ENG = ("pe", "act", "dve", "pool", "sp")

    def __init__(self, nc, es):
        self.nc = nc
        self.es = es
        self.ops = {e: [] for e in self.ENG}
        self.esem = {e: es.enter_context(nc.semaphore("s_" + e)) for e in self.ENG}
        self.dma_sems = []

    def dma_sem(self, name):
        s = self.es.enter_context(self.nc.semaphore(name))
        d = {"sem": s, "count": 0}
        self.dma_sems.append(d)
        return d

    def _deps(self, reads, writes):
        deps = []
        for r in reads:
            if r.writer is not None:
                deps.append(r.writer)
        for w in writes:
            if w.writer is not None:
                deps.append(w.writer)
            deps.extend(w.readers)
        return deps

    def _commit(self, tok, reads, writes):
        for r in reads:
            r.readers.append(tok)
        for w in writes:
            w.writer = tok
            w.readers = []

    def op(self, eng, fn, reads=(), writes=()):
        lst = self.ops[eng]
        tok = ("c", eng, len(lst))
        lst.append({"fn": fn, "deps": self._deps(reads, writes), "signal": False, "dma": None})
        self._commit(tok, reads, writes)
        return tok

    def dma(self, eng, fn, dsem, reads=(), writes=()):
        lst = self.ops[eng]
        deps = self._deps(reads, writes)
        dsem["count"] += 16
        tok = ("d", dsem, dsem["count"])
        lst.append({"fn": fn, "deps": deps, "signal": False, "dma": dsem})
        self._commit(tok, reads, writes)
        return tok

    def wait_all(self, eng, toks):
        self.ops[eng].append({"fn": None, "deps": list(toks), "signal": False, "dma": None})

    def fence(self, engines=None):
        toks = []
        for e in self.ENG:
            lst = self.ops[e]
            for i in range(len(lst) - 1, -1, -1):
                if lst[i]["fn"] is not None and lst[i]["dma"] is None:
                    toks.append(("c", e, i))
                    break
        for d in self.dma_sems:
            if d["count"] > 0:
                toks.append(("d", d, d["count"]))
        for e in (engines or self.ENG):
            if e != "pool":
                self.wait_all(e, toks)

    def _skip(self, e, i, d):
        de, di = d[1], d[2]
        if de == e and e == "pe":
            return True
        return False

    def emit(self):
        nc = self.nc
        for e in self.ENG:
            for i, o in enumerate(self.ops[e]):
                for d in o["deps"]:
                    if d[0] == "c" and not self._skip(e, i, d):
                        self.ops[d[1]][d[2]]["signal"] = True
        cum = {}
        for e in self.ENG:
            c = 0
            arr = []
            for o in self.ops[e]:
                if o["signal"]:
                    c += 1
                arr.append(c)
            cum[e] = arr
        self.nwaits = 0
        self.ninst = 0

        def emit_engine(e, eo):
            waited = {}
            for i, o in enumerate(self.ops[e]):
                need = {}
                for d in o["deps"]:
                    if d[0] == "c":
                        if self._skip(e, i, d):
                            continue
                        key = ("c", d[1])
                        val = cum[d[1]][d[2]]
                        sem = self.esem[d[1]]
                    else:
                        key = ("d", id(d[1]))
                        val = d[2]
                        sem = d[1]["sem"]
                    if waited.get(key, 0) >= val:
                        continue
                    if key not in need or need[key][1] < val:
                        need[key] = (sem, val)
                for key, (sem, val) in need.items():
                    eo.wait_ge(sem, val)
                    waited[key] = val
                    self.nwaits += 1
                if o["fn"] is None:
                    continue
                ins = o["fn"](eo)
                self.ninst += 1
                if o["dma"] is not None:
                    ins.then_inc(o["dma"]["sem"], 16)
                elif o["signal"]:
                    ins.then_inc(self.esem[e], 1)

        with nc.Block() as block:
            @block.tensor
            def _(eo):
                emit_engine("pe", eo)

            @block.scalar
            def _(eo):
                emit_engine("act", eo)

            @block.vector
            def _(eo):
                emit_engine("dve", eo)

            @block.gpsimd
            def _(eo):
                emit_engine("pool", eo)

            @block.sync
            def _(eo):
                emit_engine("sp", eo)


class Ring:
    def __init__(self, items):
        self.items = items
        self.i = 0

    def next(self):
        it = self.items[self.i % len(self.items)]
        self.i += 1
        return it


class Buf:
    def __init__(self, t, name):
        self.t = t
        self.r = Res(name)


def t5_bucket_np(rel):
    rel = np.asarray(rel, dtype=np.int64)
    half = 16
    max_exact = 8
    ret = np.where(rel > 0, half, 0)
    n = np.abs(rel)
    nf = np.maximum(n, 1).astype(np.float32)
    large = max_exact + (np.log(nf / np.float32(max_exact)) / np.float32(math.log(1024 / max_exact))
                         * np.float32(half - max_exact)).astype(np.int32)
    large = np.minimum(large, half - 1)
    return ret + np.where(n < max_exact, n, large)


def t5_bucket_exact(rel):
    rel = np.asarray(rel, dtype=np.int64)
    half, max_exact = 16, 8
    ret = np.where(rel > 0, half, 0)
    n = np.abs(rel)
    nf = np.maximum(n, 1).astype(np.float32)
    v = np.log(nf / np.float32(max_exact)).astype(np.float32) / np.float32(math.log(1024 / max_exact))
    v = (v.astype(np.float32) * np.float32(half - max_exact)).astype(np.float32)
    large = max_exact + v.astype(np.int32)
    large = np.minimum(large, half - 1)
    return ret + np.where(n < max_exact, n, large)


def host_constants():
    j = np.arange(2304)
    b = t5_bucket_exact(1151 - j)
    oh_diff = np.zeros((64, 2304), np.float32)
    oh_diff[b, j] = 1.0
    oh_dil = np.zeros((64, 3 * 384), np.float32)
    jj = np.arange(384)
    for g, (_, r) in enumerate(DIL):
        delta = 191 - jj
        valid = np.abs(delta) <= 64
        bb = t5_bucket_exact(delta * r)
        rows = np.where(valid, bb, 32)
        oh_dil[rows, g * 384 + jj] = 1.0
    anti = np.zeros((128, 128), np.float32)
    anti[np.arange(128), 127 - np.arange(128)] = 1.0
    return oh_diff, oh_dil, anti


def fm(w):
    K, N = w.shape
    return np.ascontiguousarray(w.reshape(K // 128, 128, N).transpose(1, 0, 2).reshape(128, (K // 128) * N))


def pack_weights(w_in, w_mem_kv, w_br_diff, w_br_dil, w_br_mem, w_out):
    units = np.empty((NL * NU, 128, UNIT), np.float32)
    for l in range(NL):
        wi = w_in[l]
        u = l * NU
        k = 0

        def put(mat):
            nonlocal k
            units[u + k] = fm(mat)
            k += 1

        def round_units(wbr, br):
            if wbr.shape[0] == 1024:
                for g in range(2):
                    put(wbr[:, g * 512:(g + 1) * 512])
            else:
                put(wbr)
            for g in range(2):
                put(wi[:, 10240 + br * 1024 + g * 512: 10240 + br * 1024 + (g + 1) * 512])
            for g in range(2):
                put(w_out[l][:, g * 512:(g + 1) * 512])

        for h in range(4):
            put(np.concatenate([wi[:, 9216 + h * 128: 9216 + (h + 1) * 128],
                                wi[:, 9728 + h * 128: 9728 + (h + 1) * 128],
                                w_mem_kv[l][:, h * 128:(h + 1) * 128],
                                w_mem_kv[l][:, 512 + h * 128: 512 + (h + 1) * 128]], axis=1))
        for h in range(4):
            for g in range(3):
                c = g * 512 + h * 128
                put(np.concatenate([wi[:, 4096 + c: 4096 + c + 128],
                                    wi[:, 5632 + c: 5632 + c + 128],
                                    wi[:, 7168 + c: 7168 + c + 128],
                                    wi[:, 8704 + h * 128: 8704 + (h + 1) * 128]], axis=1))
        for g in range(2):
            put(w_br_mem[l])
            put(wi[:, 10240 + 2 * 1024 + g * 512: 10240 + 2 * 1024 + (g + 1) * 512])
            put(w_br_dil[l])
            put(wi[:, 10240 + 1 * 1024 + g * 512: 10240 + 1 * 1024 + (g + 1) * 512])
        for g in range(2):
            put(w_out[l][:, g * 512:(g + 1) * 512])
        for h in range(8):
            put(np.concatenate([wi[:, h * 64:(h + 1) * 64], wi[:, 512 + h * 64: 512 + (h + 1) * 64],
                                wi[:, 1024 + h * 64: 1024 + (h + 1) * 64],
                                wi[:, 1536 + h * 64: 1536 + (h + 1) * 64],
                                wi[:, 2048 + h * 128: 2048 + (h + 1) * 128],
                                wi[:, 3072 + h * 128: 3072 + (h + 1) * 128]], axis=1))
        round_units(w_br_diff[l], 0)
        assert k == NU, k
    return units


def build_program(layers=(0, 1, 2, 3), do_final=True, branches=("M", "B", "A")):
    nc = bass.Bass("TRN2", target_bir_lowering=False)
    xT_d = nc.dram_tensor("xT", [128, 8, S_LEN], F32, kind="ExternalInput").ap()
    memT_d = nc.dram_tensor("memT", [128, 8, NMEM], F32, kind="ExternalInput").ap()
    wts_d = nc.dram_tensor("wts", [NL * NU, 128, UNIT], F32, kind="ExternalInput").ap()
    gv_d = nc.dram_tensor("gv", [128, 72], F32, kind="ExternalInput").ap()
    lam_d = nc.dram_tensor("lam", [1, NL * 256], F32, kind="ExternalInput").ap()
    relb_d = nc.dram_tensor("relb", [32, 20], F32, kind="ExternalInput").ap()
    ohd_d = nc.dram_tensor("ohdiff", [64, 2304], F32, kind="ExternalInput").ap()
    ohl_d = nc.dram_tensor("ohdil", [64, 1152], F32, kind="ExternalInput").ap()
    anti_d = nc.dram_tensor("anti", [128, 128], F32, kind="ExternalInput").ap()
    ident_d = nc.dram_tensor("ident", [128, 128], F32, kind="ExternalInput").ap()
    out_d = nc.dram_tensor("outT", [128, 8, S_LEN], F32, kind="ExternalOutput").ap()
    Fd_diff = nc.dram_tensor("Fd_diff", [20, 2304], F32)
    Fd_dil = nc.dram_tensor("Fd_dil", [20, 1152], F32)
    Ed = nc.dram_tensor("Ed", [8, 128, 2176], BF16)

    with ExitStack() as es:
        S = Sched(nc, es)
        sbuf_off = [16384]

        def salloc(name, shape, dt, at=None):
            nb = int(np.prod(shape[1:])) * (4 if dt == F32 else 2)
            if at is None:
                at = sbuf_off[0]
                sbuf_off[0] = (at + nb + 31) // 32 * 32
            return nc.alloc_sbuf_tensor_at(name, list(shape), dt, offset=at)

        xT = salloc("xTs", [128, 8, S_LEN], F32)
        hT = salloc("hTs", [128, 8, S_LEN], BF16)
        uT = salloc("uTs", [128, 8, S_LEN], BF16)
        wslots = [Buf(salloc("w%d" % i, [128, UNIT], BF16), "w%d" % i) for i in range(3)]
        gv = salloc("gvs", [128, 72], F32)
        neglam = salloc("neglam", [128, NL], F32)
        biasfar = salloc("biasfar", [128, 8, 2], F32)
        dilmask = salloc("dilmask", [128, 12, 256], BF16)
        rstd_mem = salloc("rstd_mem", [128, NMEM], F32)
        ones_bf = salloc("ones_bf", [128, 128], BF16)
        ones_fD = salloc("ones_fD", [128, 128], F32)
        ones_fH = salloc("ones_fH", [128, 128], F32)
        epsb = salloc("epsb", [128, 1], F32)
        ident_bf = salloc("ident_bf", [128, 128], BF16)
        A0 = sbuf_off[0]
        assert A0 + 47744 <= 229376, A0
        H0 = 16384 + 65536

        def aalloc(name, shape, dt, off):
            return salloc(name, shape, dt, at=A0 + off)

        def halloc(name, shape, dt, off):
            return salloc(name, shape, dt, at=H0 + off)

        qT = aalloc("qT", [128, S_LEN], BF16, 0)
        kT = aalloc("kT", [128, S_LEN], BF16, 4096)
        vT = aalloc("vT", [128, 16, 128], BF16, 8192)
        gT = aalloc("gT", [128, S_LEN], BF16, 12288)
        vA = aalloc("vA", [128, 16, 130], BF16, 8192)
        gTM = aalloc("gTM", [128, 16, 128], BF16, 12352)
        q2T = aalloc("q2T", [128, S_LEN], BF16, 39488)
        utm = Buf(aalloc("utm", [128, 4, 128], BF16, 20544), "utm")
        dsm = aalloc("dsm", [128, 8], F32, 47680)
        rs4 = aalloc("rs4", [128, 4], F32, 47712)
        pts = [Buf(aalloc("pt%d" % i, [128, 512], BF16, 16448 + 1024 * i), "pt%d" % i) for i in range(4)]
        pts += [Buf(aalloc("pt%d" % (4 + i), [128, 512], BF16, 43584 + 1024 * i), "pt%d" % (4 + i)) for i in range(4)]
        tmps = [Buf(aalloc("tmp%d" % i, [128, 512], F32, 20544 + 2048 * i), "tmp%d" % i) for i in range(5)]
        estr = [Buf(aalloc("estr%d" % i, [128, 2176], BF16, 30784 + 4352 * i), "estr%d" % i) for i in range(2)]
        accSB = aalloc("accSB", [128, 2, S_LEN], F32, 30784)
        memstage = aalloc("memstage", [128, 8, NMEM], F32, 30784)
        memnT = aalloc("memnT", [128, 8, NMEM], BF16, 38976)
        qB2 = aalloc("qB2", [128, S_LEN], BF16, 20544)
        kB2 = aalloc("kB2", [128, S_LEN], BF16, 24640)
        merged = aalloc("merged", [128, 8, S_LEN], BF16, 0)
        gsig = [Buf(aalloc("gsig%d" % i, [128, 512], F32, 32768 + 2048 * i), "gsig%d" % i) for i in range(2)]
        ostage = [Buf(aalloc("ostage%d" % i, [128, 512], F32, 36864 + 2048 * i), "ostage%d" % i) for i in range(4)]
        oh_diff = halloc("oh_diff", [64, 2304], F32, 0)
        oh_dil = halloc("oh_dil", [64, 1152], F32, 9216)
        Fsb = halloc("Fsb", [20, 2304], F32, 13824)
        Fsb_dil = halloc("Fsb_dil", [20, 1152], F32, 23040)
        Hk = halloc("Hk", [128, 2176], F32, 27648)
        Eo = halloc("Eo", [128, 2176], BF16, 36352)
        lamb = halloc("lamb", [128, NL * 256], F32, 40704)
        lamt = halloc("lamt", [128, 64], F32, 44800)
        dots = halloc("dots", [128, 2 * NL], F32, 54912)
        mst0 = halloc("mst0", [128, 8, NMEM], F32, 45056)
        msq = halloc("msq", [128, NMEM], F32, 53248)
        tab_aug = halloc("tab_aug", [64, 20], F32, 54272)
        antiI = halloc("antiI", [128, 128], F32, 54272 + 128)

        gen = Ring([Buf(es.enter_context(nc.psum_tensor("gen%d" % i, [128, 512], F32)), "gen%d" % i)
                    for i in range(4)])
        acc4 = es.enter_context(nc.psum_tensor("acc4", [128, 4, 512], F32))
        accs = [Buf(acc4[:, i, :], "acc%d" % i) for i in range(4)]
        accring = Ring(accs)
        ptring = Ring(pts)
        ptring4 = Ring(pts[:4])

        RxT = [[Res("xT%d_%d" % (c, b)) for b in range(4)] for c in range(8)]
        RhT = [Res("hT%d" % b) for b in range(4)]
        RuT = [[Res("uT%d_%d" % (c, b)) for b in range(4)] for c in range(8)]
        Rq = [Res("q%d" % b) for b in range(4)]
        Rk = [Res("k%d" % b) for b in range(4)]
        Rq2 = [Res("q2_%d" % b) for b in range(4)]
        RqB2 = [Res("qB2_%d" % b) for b in range(4)]
        RkB2 = [Res("kB2_%d" % b) for b in range(4)]
        Rg = [Res("g%d" % b) for b in range(4)]
        Rv = [Res("v%d" % b) for b in range(4)]
        Rmerged = [[Res("mg%d_%d" % (c, b)) for b in range(4)] for c in range(8)]
        Rconst = Res("const")
        Rmisc = Res("misc")
        RaccSB = Res("accSB")
        Rmemn = Res("memn")
        Rmemst = Res("memstage")
        RFd = Res("Fd")
        REd = Res("Ed")
        Routs = [Res("out%d" % i) for i in range(4)]
        d_const = S.dma_sem("d_const")
        d_x = S.dma_sem("d_x")
        d_w = [S.dma_sem("d_w%d" % i) for i in range(3)]
        d_e = [S.dma_sem("d_e%d" % i) for i in range(2)]
        d_fd = S.dma_sem("d_fd")
        d_ident = S.dma_sem("d_ident")
        d_hk = S.dma_sem("d_hk")
        d_ed = S.dma_sem("d_ed")
        d_mem = S.dma_sem("d_mem")
        d_outs = [S.dma_sem("d_out%d" % i) for i in range(4)]

        blk = lambda b: slice(b * 512, (b + 1) * 512)

        def MM(out, lhsT, rhs, start, stop, reads, writes):
            return S.op("pe", lambda e: e.matmul(out, lhsT, rhs, start=start, stop=stop), reads, writes)

        def ACT(out, in_, func, reads, writes, bias=None, scale=1.0):
            if bias is None:
                return S.op("act", lambda e: e.activation(out, in_, func, scale=scale), reads, writes)
            return S.op("act", lambda e: e.activation(out, in_, func, bias=bias, scale=scale), reads, writes)

        def TT(out, in0, in1, op, reads, writes, eng="dve"):
            return S.op(eng, lambda e: e.tensor_tensor(out, in0, in1, op), reads, writes)

        def STT(out, in0, scalar, in1, op0, op1, reads, writes, eng="dve"):
            return S.op(eng, lambda e: e.scalar_tensor_tensor(out, in0, scalar, in1, op0, op1), reads, writes)

        def TCOPY(out, in_, reads, writes, eng="dve"):
            return S.op(eng, lambda e: e.tensor_copy(out, in_), reads, writes)

        def RECIP(out, in_, reads, writes):
            return S.op("dve", lambda e: e.reciprocal(out, in_), reads, writes)

        def MEMSET(ap, val, writes, eng="dve"):
            return S.op(eng, lambda e: e.memset(ap, val), (), writes)

        def DMA(eng, out, in_, dsem, reads, writes):
            return S.dma(eng, lambda e: e.dma_start(out=out, in_=in_), dsem, reads, writes)

        wcount = [0]

        def wload(l, u):
            sl = wslots[wcount[0] % 3]
            ds = d_w[wcount[0] % 3]
            wcount[0] += 1
            DMA("pool", sl.t[:, :], wts_d[l * NU + u], ds, [], [sl.r])
            return sl

        def w8(sl):
            return sl.t[:, :].rearrange("p (c n) -> p c n", c=8)

        def w4(sl):
            return sl.t[:, :].rearrange("p (c n) -> p c n", c=4)

        def rstd_from_bank(bank, ncols, dst):
            ACT(dst.t[:, 0:ncols], bank.t[:, 0:ncols], AF.Ln, [bank.r, Rmisc], [dst.r], bias=epsb[:, 0:1])
            ACT(dst.t[:, 0:ncols], dst.t[:, 0:ncols], AF.Exp, [dst.r], [dst.r], scale=-0.5)

        def setup():
            DMA("sp", gv[:, :], gv_d, d_const, [], [Rconst])
            DMA("sp", lamb[:, :], lam_d.partition_broadcast(128), d_const, [], [Rconst])
            DMA("sp", antiI[:, :], anti_d, d_const, [], [Rconst])
            DMA("pool", ident_bf[:, :], ident_d, d_ident, [], [Rconst])
            DMA("sp", oh_diff[:, :], ohd_d, d_const, [], [Rconst])
            DMA("sp", oh_dil[:, :], ohl_d, d_const, [], [Rconst])
            DMA("sp", mst0[:, :, :], memT_d, d_const, [], [Rconst])
            for c in range(8):
                DMA("sp", xT[:, c, :], xT_d[:, c, :], d_x, [], RxT[c])
            Rtab = Res("tab")
            MEMSET(tab_aug[0:64, :], 0.0, [Rtab])
            MEMSET(tab_aug[32:33, :], MASKNEG, [Rtab])
            DMA("sp", tab_aug[0:32, :], relb_d, d_const, [], [Rtab])
            MEMSET(ones_bf[:, :], 1.0, [Rmisc])
            MEMSET(ones_fD[:, :], 1.0 / 1024, [Rmisc])
            MEMSET(ones_fH[:, :], 1.0 / 128, [Rmisc])
            MEMSET(epsb[:, :], EPS, [Rmisc])
            S.fence()
            for n0 in range(0, 2304, 512):
                n = min(512, 2304 - n0)
                b = gen.next()
                MM(b.t[0:20, 0:n], tab_aug[0:64, 0:20], oh_diff[0:64, n0:n0 + n], True, True, [Rconst, Rmisc], [b.r])
                TCOPY(Fsb[0:20, n0:n0 + n], b.t[0:20, 0:n], [b.r], [Rmisc])
            for n0 in range(0, 1152, 384):
                b = gen.next()
                MM(b.t[0:20, 0:384], tab_aug[0:64, 0:20], oh_dil[0:64, n0:n0 + 384], True, True, [Rconst, Rmisc],
                   [b.r])
                TCOPY(Fsb_dil[0:20, n0:n0 + 384], b.t[0:20, 0:384], [b.r], [Rmisc])
            DMA("sp", Fd_diff.ap(), Fsb[:, :], d_fd, [Rmisc], [RFd])
            DMA("sp", Fd_dil.ap(), Fsb_dil[:, :], d_fd, [Rmisc], [RFd])
            RH = Res("Hk")
            REo = Res("Eo")
            Rbf = Res("biasfar")
            Rdm = Res("dilmask")
            for h in range(8):
                src = bass.AP(Fd_diff, h * 2304, [[1, 128], [1, 2176]])
                DMA("sp", Hk[:, :], src, d_hk, [RFd], [RH])
                for n0 in range(0, 2176, 512):
                    n = min(512, 2176 - n0)
                    b = gen.next()
                    MM(b.t[:, 0:n], antiI[:, :], Hk[:, n0:n0 + n], True, True, [RH, Rconst], [b.r])
                    ACT(Eo[:, n0:n0 + n], b.t[:, 0:n], AF.Exp, [b.r], [REo])
                    if n0 == 0:
                        TCOPY(biasfar[:, h, 1:2], b.t[:, 0:1], [b.r], [Rbf])
                    if n0 + n == 2176:
                        TCOPY(biasfar[:, h, 0:1], b.t[:, n - 1:n], [b.r], [Rbf])
                DMA("sp", Ed[h], Eo[:, :], d_ed, [REo], [REd])
            for g in range(3):
                for h in range(4):
                    gi = g * 4 + h
                    src = bass.AP(Fd_dil, (8 + gi) * 1152 + g * 384, [[1, 128], [1, 256]])
                    DMA("sp", Hk[:, 0:256], src, d_hk, [RFd], [RH])
                    b = gen.next()
                    MM(b.t[:, 0:256], antiI[:, :], Hk[:, 0:256], True, True, [RH, Rconst], [b.r])
                    if g == 2:
                        ACT(dilmask[:, gi, 0:128], b.t[:, 64:192], AF.Exp, [b.r], [Rdm])
                    else:
                        ACT(dilmask[:, gi, 0:128], b.t[:, 128:256], AF.Exp, [b.r], [Rdm])
                        ACT(dilmask[:, gi, 128:256], b.t[:, 0:128], AF.Exp, [b.r], [Rdm])
            b = gen.next()
            Rmsq = Res("msq")
            for c in range(8):
                ACT(msq[:, :], mst0[:, c, :], AF.Square, [Rconst], [Rmsq])
                MM(b.t[:, 0:NMEM], ones_fD[:, :], msq[:, :], c == 0, c == 7, [Rmsq, Rmisc], [b.r])
            rstd_from_bank(b, NMEM, Buf(rstd_mem, "rstd_mem"))
            Rl = Res("lam")
            for l in range(NL):
                for j in range(2):
                    o = l * 256 + j * 128
                    TT(lamt[:, :], lamb[:, o:o + 64], lamb[:, o + 64:o + 128], ALU.mult, [Rconst], [Rl])
                    dst = dots[:, 2 * l + j:2 * l + j + 1]
                    S.op("dve", lambda e, dst=dst: e.reduce_sum(dst, lamt[:, :], axis=AX.X), [Rl], [Rl])
            ACT(dots[:, :], dots[:, :], AF.Exp, [Rl], [Rl])
            for l in range(NL):
                lam_init = 0.8 - 0.6 * math.exp(-0.3 * l)
                TT(neglam[:, l:l + 1], dots[:, 2 * l + 1:2 * l + 2], dots[:, 2 * l:2 * l + 1], ALU.subtract,
                   [Rl], [Rl])
                nl_ = neglam[:, l:l + 1]
                S.op("dve", lambda e, nl_=nl_, li=lam_init: e.tensor_scalar_add(nl_, nl_, -li), [Rl], [Rl])
            S.fence()

        def ln_rstd(b_):
            bank = gen.next()
            for c in range(8):
                t = tmps[c % 2]
                ACT(t.t[:, :], xT[:, c, blk(b_)], AF.Square, [RxT[c][b_]], [t.r])
                MM(bank.t[:, :], ones_fD[:, :], t.t[:, :], c == 0, c == 7, [t.r], [bank.r])
            rs = tmps[2 + (b_ % 2)]
            rstd_from_bank(bank, 512, rs)
            return rs

        def layer_norm(l):
            for b_ in range(4):
                rs = ln_rstd(b_)
                for c in range(8):
                    STT(hT[:, c, blk(b_)], xT[:, c, blk(b_)], gv[:, l * 8 + c:l * 8 + c + 1], rs.t[:, :],
                        ALU.mult, ALU.mult, [RxT[c][b_], rs.r], [RhT[b_]])

        def final_norm():
            for b_ in range(4):
                rs = ln_rstd(b_)
                for c in range(8):
                    si = (b_ * 8 + c) % 4
                    st = ostage[si]
                    STT(st.t[:, :], xT[:, c, blk(b_)], gv[:, 64 + c:64 + c + 1], rs.t[:, :], ALU.mult, ALU.mult,
                        [RxT[c][b_], rs.r], [st.r])
                    DMA("sp", out_d[:, c, blk(b_)], st.t[:, :], d_outs[si], [st.r], [Routs[si]])
            S.wait_all("sp", [("d", d, d["count"]) for d in d_outs])

        bg_i = [0]

        def next_group():
            g = gen.items if bg_i[0] % 2 == 0 else accs
            bg_i[0] += 1
            return g

        def recip_act(dst_ap, src_ap, reads, writes):
            ACT(dst_ap, src_ap, AF.Ln, reads, writes)
            ACT(dst_ap, dst_ap, AF.Exp, writes, writes, scale=-1.0)

        def proj_fm(sl, col0, dst, dstR, kind, dst2=None, dst2R=None, banks=None):
            W = w8(sl)
            if banks is None:
                banks = next_group()
            for c in range(8):
                for b_ in range(4):
                    MM(banks[b_].t[:, :], W[:, c, col0:col0 + 128], hT[:, c, blk(b_)], c == 0, c == 7,
                       [sl.r, RhT[b_]], [banks[b_].r])
            for b_ in range(4):
                bank = banks[b_]
                if kind == "silu":
                    ACT(dst[:, blk(b_)], bank.t[:, :], AF.Silu, [bank.r], [dstR[b_]])
                elif kind == "copy_act":
                    ACT(dst[:, blk(b_)], bank.t[:, :], AF.Copy, [bank.r], [dstR[b_]])
                elif kind == "qpad":
                    ACT(dst[0:64, blk(b_)], bank.t[0:64, :], AF.Copy, [bank.r], [dstR[b_]])
                    TCOPY(dst2[64:128, blk(b_)], bank.t[64:128, :], [bank.r], [dst2R[b_]])
                else:
                    TCOPY(dst[:, blk(b_)], bank.t[:, :], [bank.r], [dstR[b_]])

        def proj_v(sl, col0, tok_slices, dst=None, dstR=None, silu=False):
            W = w8(sl)
            if dst is None:
                dst, dstR = vT, Rv
            for t4 in range(4):
                bank = gen.next()
                for tt in range(4):
                    ts_ = tok_slices[t4 * 4 + tt]
                    for c in range(8):
                        MM(bank.t[:, tt * 128:(tt + 1) * 128], hT[:, c, ts_], W[:, c, col0:col0 + 128],
                           c == 0, c == 7, [sl.r] + RhT, [bank.r])
                src = bank.t[:, :].rearrange("p (t n) -> p t n", t=4)
                if silu:
                    ACT(dst[:, t4 * 4:(t4 + 1) * 4, 0:128], src, AF.Silu, [bank.r], [dstR[t4]])
                else:
                    TCOPY(dst[:, t4 * 4:(t4 + 1) * 4, 0:128], src, [bank.r], [dstR[t4]])

        def attn_run(tiles, scale, before_drain=None):
            LAG = 2
            pend = []

            def flush_one():
                t, pt = pend.pop(0)
                nq = t["nq"]
                nb, na = t["num"]
                db, da = t["den"]
                MM(na, t["v"], pt.t[:, 0:nq], t["first"], t["last"], [pt.r] + t["vR"], [nb.r])
                seg = t.get("seg")
                if seg is None:
                    MM(da, ones_bf[:, :], pt.t[:, 0:nq], t["first"], t["last"], [pt.r], [db.r])
                else:
                    seg.append(pt)
                    if t["last"]:
                        for k_, p_ in enumerate(seg):
                            MM(da, ones_bf[:, :], p_.t[:, 0:nq], k_ == 0, k_ == len(seg) - 1, [p_.r], [db.r])
                if t.get("post") is not None:
                    t["post"]()

            for t in tiles:
                st = gen.next()
                nq = t["nq"]
                MM(st.t[:, 0:nq], t["k"], t["q"], True, True, t["kqR"], [st.r])
                pt = ptring4.next()
                if t["mode"] == "const":
                    ACT(pt.t[:, 0:nq], st.t[:, 0:nq], AF.Exp, [st.r], [pt.r], bias=t["bias"], scale=scale)
                else:
                    ACT(pt.t[:, 0:nq], st.t[:, 0:nq], AF.Exp, [st.r], [pt.r], scale=scale)
                    if t["mode"] == "mul":
                        TT(pt.t[:, 0:nq], pt.t[:, 0:nq], t["bias"], ALU.mult, [pt.r] + t["biasR"], [pt.r])
                pend.append((t, pt))
                if len(pend) > LAG:
                    flush_one()
            if before_drain is not None:
                before_drain()
            while pend:
                flush_one()

        def merge_round(l, ubase, nch):
            if nch == 8:
                gate_u = ubase + 2
            else:
                wbr_single = wload(l, ubase)
                gate_u = ubase + 1
            out_u = gate_u + 2
            for ecg in range(2):
                if nch == 8:
                    wb = wload(l, ubase + ecg)
                    Wb = w8(wb)
                    cbase = 0
                else:
                    wb = wbr_single
                    Wb = w4(wb)
                    cbase = ecg * 512
                gs = wload(l, gate_u + ecg)
                G = w8(gs)
                for ecl in range(4):
                    ec = ecg * 4 + ecl
                    cs = slice(cbase + ecl * 128, cbase + (ecl + 1) * 128)
                    for half in range(2):
                        bl = (2 * half, 2 * half + 1)
                        yb = [gen.next(), gen.next()]
                        for c in range(nch):
                            for k_, b_ in enumerate(bl):
                                MM(yb[k_].t[:, :], Wb[:, c, cs], uT[:, c, blk(b_)], c == 0, c == nch - 1,
                                   [wb.r, RuT[c][b_]], [yb[k_].r])
                        gb = [accring.next(), accring.next()]
                        for c in range(8):
                            for k_, b_ in enumerate(bl):
                                MM(gb[k_].t[:, :], G[:, c, ecl * 128:(ecl + 1) * 128], hT[:, c, blk(b_)], c == 0,
                                   c == 7, [gs.r, RhT[b_]], [gb[k_].r])
                        for k_, b_ in enumerate(bl):
                            gsb = gsig[k_]
                            ACT(gsb.t[:, :], gb[k_].t[:, :], AF.Sigmoid, [gb[k_].r], [gsb.r])
                            TT(merged[:, ec, blk(b_)], yb[k_].t[:, :], gsb.t[:, :], ALU.mult, [yb[k_].r, gsb.r],
                               [Rmerged[ec][b_]])
            for ocg in range(2):
                ws = wload(l, out_u + ocg)
                Wo = w8(ws)
                for ocl in range(4):
                    oc = ocg * 4 + ocl
                    obs = next_group()
                    for c in range(8):
                        for b_ in range(4):
                            MM(obs[b_].t[:, :], Wo[:, c, ocl * 128:(ocl + 1) * 128], merged[:, c, blk(b_)], c == 0,
                               c == 7, [ws.r, Rmerged[c][b_]], [obs[b_].r])
                    for b_ in range(4):
                        TT(xT[:, oc, blk(b_)], obs[b_].t[:, :], xT[:, oc, blk(b_)], ALU.add,
                           [obs[b_].r, RxT[oc][b_]], [RxT[oc][b_]])
            S.fence(("act", "dve"))

        def merge_round_MB(l, ubase):
            mtmp = ostage[0]
            for ecg in range(2):
                for part in range(2):
                    wb = wload(l, ubase + 4 * ecg + 2 * part)
                    gs = wload(l, ubase + 4 * ecg + 2 * part + 1)
                    Wb = w4(wb)
                    G = w8(gs)
                    c0 = 4 * part
                    for ecl in range(4):
                        ec = ecg * 4 + ecl
                        cs = slice(ecg * 512 + ecl * 128, ecg * 512 + (ecl + 1) * 128)
                        for half in range(2):
                            bl = (2 * half, 2 * half + 1)
                            yb = [gen.next(), gen.next()]
                            for c in range(4):
                                for k_, b_ in enumerate(bl):
                                    MM(yb[k_].t[:, :], Wb[:, c, cs], uT[:, c0 + c, blk(b_)], c == 0, c == 3,
                                       [wb.r, RuT[c0 + c][b_]], [yb[k_].r])
                            gb = [accring.next(), accring.next()]
                            for c in range(8):
                                for k_, b_ in enumerate(bl):
                                    MM(gb[k_].t[:, :], G[:, c, ecl * 128:(ecl + 1) * 128], hT[:, c, blk(b_)], c == 0,
                                       c == 7, [gs.r, RhT[b_]], [gb[k_].r])
                            for k_, b_ in enumerate(bl):
                                gsb = gsig[k_]
                                ACT(gsb.t[:, :], gb[k_].t[:, :], AF.Sigmoid, [gb[k_].r], [gsb.r])
                                if part == 0:
                                    TT(merged[:, ec, blk(b_)], yb[k_].t[:, :], gsb.t[:, :], ALU.mult,
                                       [yb[k_].r, gsb.r], [Rmerged[ec][b_]])
                                else:
                                    TT(mtmp.t[:, :], yb[k_].t[:, :], gsb.t[:, :], ALU.mult, [yb[k_].r, gsb.r],
                                       [mtmp.r])
                                    TT(merged[:, ec, blk(b_)], merged[:, ec, blk(b_)], mtmp.t[:, :], ALU.add,
                                       [Rmerged[ec][b_], mtmp.r], [Rmerged[ec][b_]])
            for ocg in range(2):
                ws = wload(l, ubase + 8 + ocg)
                Wo = w8(ws)
                for ocl in range(4):
                    oc = ocg * 4 + ocl
                    obs = next_group()
                    for c in range(8):
                        for b_ in range(4):
                            MM(obs[b_].t[:, :], Wo[:, c, ocl * 128:(ocl + 1) * 128], merged[:, c, blk(b_)], c == 0,
                               c == 7, [ws.r, Rmerged[c][b_]], [obs[b_].r])
                    for b_ in range(4):
                        TT(xT[:, oc, blk(b_)], obs[b_].t[:, :], xT[:, oc, blk(b_)], ALU.add,
                           [obs[b_].r, RxT[oc][b_]], [RxT[oc][b_]])
            S.fence(("act", "dve", "sp"))

        def branch_M(l):
            DMA("sp", memstage[:, :, :], memT_d, d_mem, [], [Rmemst])
            for c in range(8):
                STT(memnT[:, c, :], memstage[:, c, :], gv[:, 32 + l * 8 + c:32 + l * 8 + c + 1], rstd_mem[:, :],
                    ALU.mult, ALU.mult, [Rmemst], [Rmemn])
            scale = 128 ** -0.5
            for h in range(4):
                sl = wload(l, h)
                W = w8(sl)
                proj_fm(sl, 0, qT, Rq, "copy_act")
                proj_fm(sl, 128, gT, Rg, "silu")
                bank = gen.next()
                for c in range(8):
                    MM(bank.t[:, 0:NMEM], W[:, c, 256:384], memnT[:, c, :], c == 0, c == 7, [sl.r, Rmemn], [bank.r])
                TCOPY(kT[:, 0:NMEM], bank.t[:, 0:NMEM], [bank.r], [Rk[0]])
                bank = gen.next()
                for tt in range(2):
                    for c in range(8):
                        MM(bank.t[:, tt * 128:(tt + 1) * 128], memnT[:, c, tt * 128:(tt + 1) * 128],
                           W[:, c, 384:512], c == 0, c == 7, [sl.r, Rmemn], [bank.r])
                TCOPY(vT[:, 0:2, :], bank.t[:, 0:256].rearrange("p (t n) -> p t n", t=2), [bank.r], [Rv[0]])
                tiles = []
                for J in range(4):
                    nb = accs[(J % 2) * 2]
                    db = accs[(J % 2) * 2 + 1]
                    r1, t1 = tmps[J % 2], tmps[2 + J % 2]

                    def post(J=J, nb=nb, db=db, r1=r1, t1=t1, h=h):
                        recip_act(r1.t[:, :], db.t[:, :], [db.r], [r1.r])
                        TT(t1.t[:, :], nb.t[:, :], r1.t[:, :], ALU.mult, [nb.r, r1.r], [t1.r])
                        TT(uT[:, h, blk(J)], t1.t[:, :], gT[:, blk(J)], ALU.mult, [t1.r, Rg[J]], [RuT[h][J]])

                    for i in range(2):
                        tiles.append(dict(k=kT[:, i * 128:(i + 1) * 128], q=qT[:, blk(J)], nq=512, mode="none",
                                          kqR=[Rk[0], Rq[J]], v=vT[:, i, :], vR=[Rv[0]],
                                          num=(nb, nb.t[:, :]), den=(db, db.t[:, :]), first=(i == 0), last=(i == 1),
                                          post=(post if i == 1 else None)))
                attn_run(tiles, scale)
            S.fence(("act", "dve"))

        def branch_B(l):
            scale = 128 ** -0.5
            bsets = [(qT, kT, Rq, Rk), (qB2, kB2, RqB2, RkB2)]
            units = [(h_, g_) for h_ in range(4) for g_ in range(3)]
            slots = {0: wload(l, 4)}

            def proj_qk(n_):
                q__, k__, Rq__, Rk__ = bsets[n_ % 2]
                proj_fm(slots[n_], 0, q__, Rq__, "copy_act", banks=gen.items)
                proj_fm(slots[n_], 128, k__, Rk__, "copy_dve", banks=gen.items)

            proj_qk(0)
            for h in range(4):
                for g, (_, r) in enumerate(DIL):
                    L = S_LEN // r
                    nseg = L // 128
                    n_u = h * 3 + g
                    sl = slots[n_u]
                    qs_, ks_, Rqs_, Rks_ = bsets[n_u % 2]
                    if g == 0:
                        proj_fm(sl, 384, gT, Rg, "silu")
                    toks = []
                    for c in range(r):
                        for j in range(nseg):
                            s0 = c + r * 128 * j
                            toks.append(slice(s0, s0 + r * 127 + 1, r))
                    proj_v(sl, 256, toks)
                    gi = g * 4 + h
                    tiles = []
                    for c in range(r):
                        for s in (range(nseg + 1) if g < 2 else (0,)):
                            if g < 2:
                                qlo = max(0, 128 * s - 64)
                                qhi = min(L, 128 * s + 64)
                                js = [j for j in (s - 1, s) if 0 <= j < nseg]
                            else:
                                qlo, qhi, js = 0, L, [0]
                            nq = qhi - qlo
                            o = qlo - (128 * s - 64)
                            ab = accring.next()
                            qsl = slice(c + r * qlo, c + r * (qhi - 1) + 1, r)
                            seg = []

                            def post(ab=ab, nq=nq, qsl=qsl, g=g):
                                src = ab.t[:, :].rearrange("p (a n) -> p a n", a=2)[:, :, 0:nq]
                                dst = accSB[:, :, qsl]
                                if g == 0:
                                    ACT(dst, src, AF.Copy, [ab.r], [RaccSB])
                                else:
                                    TT(dst, src, dst, ALU.add, [ab.r, RaccSB], [RaccSB])

                            for idx, j in enumerate(js):
                                ksl = slice(c + r * 128 * j, c + r * 128 * j + r * 127 + 1, r)
                                moff = ((0 if j == s - 1 else 128) + o) if g < 2 else 0
                                tiles.append(dict(k=ks_[:, ksl], q=qs_[:, qsl], nq=nq, mode="mul",
                                                  bias=dilmask[:, gi, moff:moff + nq], biasR=[],
                                                  kqR=Rks_ + Rqs_, v=vT[:, c * nseg + j, :], vR=Rv,
                                                  num=(ab, ab.t[:, 0:nq]), den=(ab, ab.t[:, 256:256 + nq]),
                                                  first=(idx == 0), last=(idx == len(js) - 1), seg=seg,
                                                  post=(post if idx == len(js) - 1 else None)))
                    nxt = None
                    if n_u + 1 < 12:
                        slots[n_u + 1] = wload(l, 4 + n_u + 1)
                        nxt = (lambda n_=n_u + 1: proj_qk(n_))
                    attn_run(tiles, scale, before_drain=nxt)
                recip_act(accSB[:, 1, :], accSB[:, 1, :], [RaccSB], [RaccSB])
                TT(accSB[:, 0, :], accSB[:, 0, :], accSB[:, 1, :], ALU.mult, [RaccSB], [RaccSB])
                TT(uT[:, 4 + h, :], accSB[:, 0, :], gT[:, :], ALU.mult, [RaccSB] + Rg, RuT[4 + h])
            S.fence(("act", "dve"))
            merge_round_MB(l, 16)

        def branch_A(l):
            scale = 64 ** -0.5
            cnorm = 1.0 - (0.8 - 0.6 * math.exp(-0.3 * l))
            MEMSET(qT[64:128, :], 0.0, Rq)
            MEMSET(q2T[0:64, :], 0.0, Rq2)
            MEMSET(vA[:, :, 128:129], 1.0, Rv)
            Rsm = Res("small")
            nat = [slice(t * 128, (t + 1) * 128) for t in range(16)]
            cn1, cn2 = tmps[1], tmps[2]
            c13 = cn1.t[:, :].rearrange("p (c e) -> p c e", c=4)
            c23 = cn2.t[:, :].rearrange("p (c e) -> p c e", c=4)
            accR = [a.r for a in accs]
            fin2 = []

            def pop_stage():
                if fin2:
                    fin2.pop(0)()

            def make_stages(h, J):
                bc = lambda ap: ap.unsqueeze(2).to_broadcast([128, 4, 128])

                def a1():
                    RECIP(dsm[:, :], dsm[:, :], [Rsm], [Rsm])

                def a2():
                    S.op("dve", lambda e: e.tensor_scalar_mul(dsm[:, 4:8], dsm[:, 4:8], neglam[:, l:l + 1]),
                         [Rsm], [Rsm])

                def a3():
                    TT(c13, c13, bc(dsm[:, 0:4]), ALU.mult, [cn1.r, Rsm], [cn1.r])

                def a4():
                    TT(c23, c23, bc(dsm[:, 4:8]), ALU.mult, [cn2.r, Rsm], [cn2.r])

                def a5():
                    TT(cn1.t[:, :], cn1.t[:, :], cn2.t[:, :], ALU.add, [cn1.r, cn2.r], [cn1.r])

                def b1():
                    TT(cn2.t[:, :], cn1.t[:, :], cn1.t[:, :], ALU.mult, [cn1.r], [cn2.r])

                def b2():
                    S.op("dve", lambda e: e.reduce_sum(rs4[:, :], c23, axis=AX.X), [cn2.r], [Rsm])

                def b3():
                    ACT(rs4[:, :], rs4[:, :], AF.Ln, [Rsm], [Rsm], bias=epsb[:, 0:1], scale=1.0 / 128)
                    ACT(rs4[:, :], rs4[:, :], AF.Exp, [Rsm], [Rsm], scale=-0.5)

                def c1():
                    S.op("dve", lambda e: e.tensor_scalar_mul(rs4[:, :], rs4[:, :], cnorm), [Rsm], [Rsm])

                def c2():
                    TT(c23, c13, bc(rs4[:, :]), ALU.mult, [cn1.r, Rsm], [cn2.r])

                def c3():
                    TT(utm.t[:, :, :], c23, gTM[:, 4 * J:4 * J + 4, :], ALU.mult, [cn2.r, Rg[J]], [utm.r])

                def c4():
                    tb = gen.next()
                    tbv = tb.t[:, :].bitcast(BF16)
                    for qc in range(4):
                        S.op("pe", lambda e, qc=qc: e.transpose(tbv[:, qc * 128:(qc + 1) * 128], utm.t[:, qc, :],
                                                                 ident_bf[:, :]), [utm.r], [tb.r])
                    ACT(uT[:, h, blk(J)], tbv[:, 0:512], AF.Copy, [tb.r], [RuT[h][J]])

                c3.reads_gate = True
                return [a1, a2, a3, a4, a5, b1, b2, b3, c1, c2, c3, c4]

            def pop_until_gate():
                while fin2 and any(getattr(f, "reads_gate", False) for f in fin2):
                    fin2.pop(0)()

            for h in range(8):
                sl = wload(l, 26 + h)
                es_ = estr[h % 2]
                DMA("sp", es_.t[:, :], Ed[h], d_e[h % 2], [REd], [es_.r])
                proj_fm(sl, 0, qT, Rq, "qpad", q2T, Rq2)
                for _ in range(5):
                    pop_stage()
                proj_fm(sl, 128, kT, Rk, "copy_dve")
                pop_until_gate()
                proj_v(sl, 384, nat, gTM, Rg, silu=True)
                while fin2:
                    fin2.pop(0)()
                proj_v(sl, 256, nat, vA, Rv)
                pend = []

                def flush_one():
                    J, m, i, pm = pend.pop(0)
                    for qc in range(4):
                        bank = accs[qc]
                        MM(bank.t[:, 0:129], pm.t[:, qc * 128:(qc + 1) * 128], vA[:, i, 0:129],
                           i == 0, i == 15, [pm.r, Rv[i // 4]], [bank.r])
                    if i == 15:
                        cn, c3 = (cn1, c13) if m == 0 else (cn2, c23)
                        if m == 0:
                            while fin2:
                                fin2.pop(0)()
                        TCOPY(c3, acc4[:, :, 0:128], accR, [cn.r])
                        TCOPY(dsm[:, 4 * m:4 * m + 4], acc4[:, :, 128], accR, [Rsm])
                        if m == 1:
                            fin2.extend(make_stages(h, J))
                    elif m == 0 and 1 <= i <= 13:
                        pop_stage()

                for J in range(4):
                    for m in range(2):
                        qsrc, Rqs = (qT, Rq) if m == 0 else (q2T, Rq2)
                        for i in range(16):
                            mm = i - 4 * J
                            near = -5 <= mm <= 8
                            st = gen.next()
                            MM(st.t[:, :], kT[:, i * 128:(i + 1) * 128], qsrc[:, blk(J)], True, True,
                               [Rk[i // 4], Rqs[J]], [st.r])
                            p = ptring.next()
                            if near:
                                col0 = 1024 - 128 * mm
                                ACT(p.t[:, :], st.t[:, :], AF.Exp, [st.r], [p.r], scale=scale)
                                TT(p.t[:, :], p.t[:, :], es_.t[:, col0:col0 + 512], ALU.mult, [p.r, es_.r], [p.r])
                            else:
                                fi = 1 if mm > 8 else 0
                                ACT(p.t[:, :], st.t[:, :], AF.Exp, [st.r], [p.r], bias=biasfar[:, h, fi:fi + 1],
                                    scale=scale)
                            pend.append((J, m, i, p))
                            if len(pend) > 3:
                                flush_one()
                while pend:
                    flush_one()
            while fin2:
                fin2.pop(0)()
            S.fence(("act", "dve"))
            merge_round(l, 34, 8)

        setup()
        for l in layers:
            layer_norm(l)
            S.fence()
            if "M" in branches:
                branch_M(l)
            if "B" in branches:
                branch_B(l)
            if "A" in branches:
                branch_A(l)
        if do_final:
            final_norm()
        else:
            for b_ in range(4):
                for c in range(8):
                    DMA("sp", out_d[:, c, blk(b_)], xT[:, c, blk(b_)], d_outs[0], [RxT[c][b_]], [Routs[0]])
            S.wait_all("sp", [("d", d_outs[0], d_outs[0]["count"])])
        S.emit()
        stats = dict(ninst=S.ninst, nwaits=S.nwaits, per_engine={e: len(S.ops[e]) for e in S.ENG})
    return nc, stats


def prep_inputs(x, mem, g_norm, w_in, diff_lambda, w_mem_kv, g_mem, w_br_diff, w_br_dil, w_br_mem, w_out,
                rel_bias, g_final):
    f = lambda a: np.asarray(a, dtype=np.float32)
    x, mem = f(x), f(mem)
    B = x.shape[0]
    wts = pack_weights(f(w_in), f(w_mem_kv), f(w_br_diff), f(w_br_dil), f(w_br_mem), f(w_out))
    gvec = np.zeros((128, 72), np.float32)
    for l in range(NL):
        gvec[:, l * 8:(l + 1) * 8] = f(g_norm)[l].reshape(8, 128).T
        gvec[:, 32 + l * 8:32 + (l + 1) * 8] = f(g_mem)[l].reshape(8, 128).T
    gvec[:, 64:72] = f(g_final).reshape(8, 128).T
    lam = np.ascontiguousarray(f(diff_lambda).reshape(1, NL * 256))
    oh_diff, oh_dil, anti = host_constants()
    in_maps = []
    for b in range(B):
        xTb = np.ascontiguousarray(x[b].reshape(S_LEN, 8, 128).transpose(2, 1, 0))
        mTb = np.ascontiguousarray(mem[b].reshape(NMEM, 8, 128).transpose(2, 1, 0))
        in_maps.append({"xT": xTb, "memT": mTb, "wts": wts, "gv": gvec, "lam": lam,
                        "relb": np.ascontiguousarray(f(rel_bias)), "ohdiff": oh_diff, "ohdil": oh_dil,
                        "anti": anti, "ident": np.eye(128, dtype=np.float32)})
    return in_maps


def unpack_out(res_list):
    outs = []
    for r in res_list:
        o = np.asarray(r["outT"], dtype=np.float32)
        outs.append(o.transpose(2, 1, 0).reshape(S_LEN, 1024))
    return np.stack(outs, axis=0)


def kernel(x, mem, g_norm, w_in, diff_lambda, w_mem_kv, g_mem, w_br_diff, w_br_dil, w_br_mem, w_out,
           rel_bias, g_final):
    in_maps = prep_inputs(x, mem, g_norm, w_in, diff_lambda, w_mem_kv, g_mem, w_br_diff, w_br_dil,
                          w_br_mem, w_out, rel_bias, g_final)
    nc, _ = build_program()
    res = run_bass_kernel_spmd(nc, in_maps, core_ids=list(range(len(in_maps))))
    return unpack_out(res.results)
```

```python
import math
import numpy as np
from contextlib import ExitStack
import concourse.bass as bass
import concourse.mybir as mybir
from concourse.bass_utils import run_bass_kernel_spmd

F32 = mybir.dt.float32
BF16 = mybir.dt.bfloat16
ALU = mybir.AluOpType
AF = mybir.ActivationFunctionType
AX = mybir.AxisListType

NL = 4
S_LEN = 2048
NMEM = 256
DIL = ((128, 1), (512, 4), (2048, 16))
NU = 40
UNIT = 4096
EPS = 1e-6
MASKNEG = -30000.0


class Res:
    __slots__ = ("name", "writer", "readers")

    def __init__(self, name):
        self.name = name
        self.writer = None
        self.readers = []


class Sched:
    ENG = ("pe", "act", "dve", "pool", "sp")

    def __init__(self, nc, es):
        self.nc = nc
        self.es = es
        self.ops = {e: [] for e in self.ENG}
        self.esem = {e: es.enter_context(nc.semaphore("s_" + e)) for e in self.ENG}
        self.dma_sems = []

    def dma_sem(self, name):
        s = self.es.enter_context(self.nc.semaphore(name))
        d = {"sem": s, "count": 0}
        self.dma_sems.append(d)
        return d

    def _deps(self, reads, writes):
        deps = []
        for r in reads:
            if r.writer is not None:
                deps.append(r.writer)
        for w in writes:
            if w.writer is not None:
                deps.append(w.writer)
            deps.extend(w.readers)
        return deps

    def _commit(self, tok, reads, writes):
        for r in reads:
            r.readers.append(tok)
        for w in writes:
            w.writer = tok
            w.readers = []

    def op(self, eng, fn, reads=(), writes=()):
        lst = self.ops[eng]
        tok = ("c", eng, len(lst))
        lst.append({"fn": fn, "deps": self._deps(reads, writes), "signal": False, "dma": None})
        self._commit(tok, reads, writes)
        return tok

    def dma(self, eng, fn, dsem, reads=(), writes=()):
        lst = self.ops[eng]
        deps = self._deps(reads, writes)
        dsem["count"] += 16
        tok = ("d", dsem, dsem["count"])
        lst.append({"fn": fn, "deps": deps, "signal": False, "dma": dsem})
        self._commit(tok, reads, writes)
        return tok

    def wait_all(self, eng, toks):
        self.ops[eng].append({"fn": None, "deps": list(toks), "signal": False, "dma": None})

    def fence(self, engines=None):
        toks = []
        for e in self.ENG:
            lst = self.ops[e]
            for i in range(len(lst) - 1, -1, -1):
                if lst[i]["fn"] is not None and lst[i]["dma"] is None:
                    toks.append(("c", e, i))
                    break
        for d in self.dma_sems:
            if d["count"] > 0:
                toks.append(("d", d, d["count"]))
        for e in (engines or self.ENG):
            if e != "pool":
                self.wait_all(e, toks)

    def _skip(self, e, i, d):
        de, di = d[1], d[2]
        if de == e and e == "pe":
            return True
        return False

    def emit(self):
        nc = self.nc
        for e in self.ENG:
            for i, o in enumerate(self.ops[e]):
                for d in o["deps"]:
                    if d[0] == "c" and not self._skip(e, i, d):
                        self.ops[d[1]][d[2]]["signal"] = True
        cum = {}
        for e in self.ENG:
            c = 0
            arr = []
            for o in self.ops[e]:
                if o["signal"]:
                    c += 1
                arr.append(c)
            cum[e] = arr
        self.nwaits = 0
        self.ninst = 0

        def emit_engine(e, eo):
            waited = {}
            for i, o in enumerate(self.ops[e]):
                need = {}
                for d in o["deps"]:
                    if d[0] == "c":
                        if self._skip(e, i, d):
                            continue
                        key = ("c", d[1])
                        val = cum[d[1]][d[2]]
                        sem = self.esem[d[1]]
                    else:
                        key = ("d", id(d[1]))
                        val = d[2]
                        sem = d[1]["sem"]
                    if waited.get(key, 0) >= val:
                        continue
                    if key not in need or need[key][1] < val:
                        need[key] = (sem, val)
                for key, (sem, val) in need.items():
                    eo.wait_ge(sem, val)
                    waited[key] = val
                    self.nwaits += 1
                if o["fn"] is None:
                    continue
                ins = o["fn"](eo)
                self.ninst += 1
                if o["dma"] is not None:
                    ins.then_inc(o["dma"]["sem"], 16)
                elif o["signal"]:
                    ins.then_inc(self.esem[e], 1)

        with nc.Block() as block:
            @block.tensor
            def _(eo):
                emit_engine("pe", eo)

            @block.scalar
            def _(eo):
                emit_engine("act", eo)

            @block.vector
            def _(eo):
                emit_engine("dve", eo)

            @block.gpsimd
            def _(eo):
                emit_engine("pool", eo)

            @block.sync
            def _(eo):
                emit_engine("sp", eo)


class Ring:
    def __init__(self, items):
        self.items = items
        self.i = 0

    def next(self):
        it = self.items[self.i % len(self.items)]
        self.i += 1
        return it


class Buf:
    def __init__(self, t, name):
        self.t = t
        self.r = Res(name)


def t5_bucket_np(rel):
    rel = np.asarray(rel, dtype=np.int64)
    half = 16
    max_exact = 8
    ret = np.where(rel > 0, half, 0)
    n = np.abs(rel)
    nf = np.maximum(n, 1).astype(np.float32)
    large = max_exact + (np.log(nf / np.float32(max_exact)) / np.float32(math.log(1024 / max_exact))
                         * np.float32(half - max_exact)).astype(np.int32)
    large = np.minimum(large, half - 1)
    return ret + np.where(n < max_exact, n, large)


def t5_bucket_exact(rel):
    rel = np.asarray(rel, dtype=np.int64)
    half, max_exact = 16, 8
    ret = np.where(rel > 0, half, 0)
    n = np.abs(rel)
    nf = np.maximum(n, 1).astype(np.float32)
    v = np.log(nf / np.float32(max_exact)).astype(np.float32) / np.float32(math.log(1024 / max_exact))
    v = (v.astype(np.float32) * np.float32(half - max_exact)).astype(np.float32)
    large = max_exact + v.astype(np.int32)
    large = np.minimum(large, half - 1)
    return ret + np.where(n < max_exact, n, large)


def host_constants():
    j = np.arange(2304)
    b = t5_bucket_exact(1151 - j)
    oh_diff = np.zeros((64, 2304), np.float32)
    oh_diff[b, j] = 1.0
    oh_dil = np.zeros((64, 3 * 384), np.float32)
    jj = np.arange(384)
    for g, (_, r) in enumerate(DIL):
        delta = 191 - jj
        valid = np.abs(delta) <= 64
        bb = t5_bucket_exact(delta * r)
        rows = np.where(valid, bb, 32)
        oh_dil[rows, g * 384 + jj] = 1.0
    anti = np.zeros((128, 128), np.float32)
    anti[np.arange(128), 127 - np.arange(128)] = 1.0
    return oh_diff, oh_dil, anti


def fm(w):
    K, N = w.shape
    return np.ascontiguousarray(w.reshape(K // 128, 128, N).transpose(1, 0, 2).reshape(128, (K // 128) * N))


def pack_weights(w_in, w_mem_kv, w_br_diff, w_br_dil, w_br_mem, w_out):
    units = np.empty((NL * NU, 128, UNIT), np.float32)
    for l in range(NL):
        wi = w_in[l]
        u = l * NU
        k = 0

        def put(mat):
            nonlocal k
            units[u + k] = fm(mat)
            k += 1

        def round_units(wbr, br):
            if wbr.shape[0] == 1024:
                for g in range(2):
                    put(wbr[:, g * 512:(g + 1) * 512])
            else:
                put(wbr)
            for g in range(2):
                put(wi[:, 10240 + br * 1024 + g * 512: 10240 + br * 1024 + (g + 1) * 512])
            for g in range(2):
                put(w_out[l][:, g * 512:(g + 1) * 512])

        for h in range(4):
            put(np.concatenate([wi[:, 9216 + h * 128: 9216 + (h + 1) * 128],
                                wi[:, 9728 + h * 128: 9728 + (h + 1) * 128],
                                w_mem_kv[l][:, h * 128:(h + 1) * 128],
                                w_mem_kv[l][:, 512 + h * 128: 512 + (h + 1) * 128]], axis=1))
        for h in range(4):
            for g in range(3):
                c = g * 512 + h * 128
                put(np.concatenate([wi[:, 4096 + c: 4096 + c + 128],
                                    wi[:, 5632 + c: 5632 + c + 128],
                                    wi[:, 7168 + c: 7168 + c + 128],
                                    wi[:, 8704 + h * 128: 8704 + (h + 1) * 128]], axis=1))
        for g in range(2):
            put(w_br_mem[l])
            put(wi[:, 10240 + 2 * 1024 + g * 512: 10240 + 2 * 1024 + (g + 1) * 512])
            put(w_br_dil[l])
            put(wi[:, 10240 + 1 * 1024 + g * 512: 10240 + 1 * 1024 + (g + 1) * 512])
        for g in range(2):
            put(w_out[l][:, g * 512:(g + 1) * 512])
        for h in range(8):
            put(np.concatenate([wi[:, h * 64:(h + 1) * 64], wi[:, 512 + h * 64: 512 + (h + 1) * 64],
                                wi[:, 1024 + h * 64: 1024 + (h + 1) * 64],
                                wi[:, 1536 + h * 64: 1536 + (h + 1) * 64],
                                wi[:, 2048 + h * 128: 2048 + (h + 1) * 128],
                                wi[:, 3072 + h * 128: 3072 + (h + 1) * 128]], axis=1))
        round_units(w_br_diff[l], 0)
        assert k == NU, k
    return units


def build_program(layers=(0, 1, 2, 3), do_final=True, branches=("M", "B", "A")):
    nc = bass.Bass("TRN2", target_bir_lowering=False)
    xT_d = nc.dram_tensor("xT", [128, 8, S_LEN], F32, kind="ExternalInput").ap()
    memT_d = nc.dram_tensor("memT", [128, 8, NMEM], F32, kind="ExternalInput").ap()
    wts_d = nc.dram_tensor("wts", [NL * NU, 128, UNIT], F32, kind="ExternalInput").ap()
    gv_d = nc.dram_tensor("gv", [128, 72], F32, kind="ExternalInput").ap()
    lam_d = nc.dram_tensor("lam", [1, NL * 256], F32, kind="ExternalInput").ap()
    relb_d = nc.dram_tensor("relb", [32, 20], F32, kind="ExternalInput").ap()
    ohd_d = nc.dram_tensor("ohdiff", [64, 2304], F32, kind="ExternalInput").ap()
    ohl_d = nc.dram_tensor("ohdil", [64, 1152], F32, kind="ExternalInput").ap()
    anti_d = nc.dram_tensor("anti", [128, 128], F32, kind="ExternalInput").ap()
    ident_d = nc.dram_tensor("ident", [128, 128], F32, kind="ExternalInput").ap()
    out_d = nc.dram_tensor("outT", [128, 8, S_LEN], F32, kind="ExternalOutput").ap()
    Fd_diff = nc.dram_tensor("Fd_diff", [20, 2304], F32)
    Fd_dil = nc.dram_tensor("Fd_dil", [20, 1152], F32)
    Ed = nc.dram_tensor("Ed", [8, 128, 2176], BF16)

    with ExitStack() as es:
        S = Sched(nc, es)
        sbuf_off = [16384]

        def salloc(name, shape, dt, at=None):
            nb = int(np.prod(shape[1:])) * (4 if dt == F32 else 2)
            if at is None:
                at = sbuf_off[0]
                sbuf_off[0] = (at + nb + 31) // 32 * 32
            return nc.alloc_sbuf_tensor_at(name, list(shape), dt, offset=at)

        xT = salloc("xTs", [128, 8, S_LEN], F32)
        hT = salloc("hTs", [128, 8, S_LEN], BF16)
        uT = salloc("uTs", [128, 8, S_LEN], BF16)
        wslots = [Buf(salloc("w%d" % i, [128, UNIT], BF16), "w%d" % i) for i in range(3)]
        gv = salloc("gvs", [128, 72], F32)
        neglam = salloc("neglam", [128, NL], F32)
        biasfar = salloc("biasfar", [128, 8, 2], F32)
        dilmask = salloc("dilmask", [128, 12, 256], BF16)
        rstd_mem = salloc("rstd_mem", [128, NMEM], F32)
        ones_bf = salloc("ones_bf", [128, 128], BF16)
        ones_fD = salloc("ones_fD", [128, 128], F32)
        ones_fH = salloc("ones_fH", [128, 128], F32)
        epsb = salloc("epsb", [128, 1], F32)
        ident_bf = salloc("ident_bf", [128, 128], BF16)
        A0 = sbuf_off[0]
        assert A0 + 47744 <= 229376, A0
        H0 = 16384 + 65536

        def aalloc(name, shape, dt, off):
            return salloc(name, shape, dt, at=A0 + off)

        def halloc(name, shape, dt, off):
            return salloc(name, shape, dt, at=H0 + off)

        qT = aalloc("qT", [128, S_LEN], BF16, 0)
        kT = aalloc("kT", [128, S_LEN], BF16, 4096)
        vT = aalloc("vT", [128, 16, 128], BF16, 8192)
        gT = aalloc("gT", [128, S_LEN], BF16, 12288)
        vA = aalloc("vA", [128, 16, 130], BF16, 8192)
        gTM = aalloc("gTM", [128, 16, 128], BF16, 12352)
        q2T = aalloc("q2T", [128, S_LEN], BF16, 39488)
        utm = Buf(aalloc("utm", [128, 4, 128], BF16, 20544), "utm")
        dsm = aalloc("dsm", [128, 8], F32, 47680)
        rs4 = aalloc("rs4", [128, 4], F32, 47712)
        pts = [Buf(aalloc("pt%d" % i, [128, 512], BF16, 16448 + 1024 * i), "pt%d" % i) for i in range(4)]
        pts += [Buf(aalloc("pt%d" % (4 + i), [128, 512], BF16, 43584 + 1024 * i), "pt%d" % (4 + i)) for i in range(4)]
        tmps = [Buf(aalloc("tmp%d" % i, [128, 512], F32, 20544 + 2048 * i), "tmp%d" % i) for i in range(5)]
        estr = [Buf(aalloc("estr%d" % i, [128, 2176], BF16, 30784 + 4352 * i), "estr%d" % i) for i in range(2)]
        accSB = aalloc("accSB", [128, 2, S_LEN], F32, 30784)
        memstage = aalloc("memstage", [128, 8, NMEM], F32, 30784)
        memnT = aalloc("memnT", [128, 8, NMEM], BF16, 38976)
        merged = aalloc("merged", [128, 8, S_LEN], BF16, 0)
        gsig = [Buf(aalloc("gsig%d" % i, [128, 512], F32, 32768 + 2048 * i), "gsig%d" % i) for i in range(2)]
        ostage = [Buf(aalloc("ostage%d" % i, [128, 512], F32, 36864 + 2048 * i), "ostage%d" % i) for i in range(4)]
        oh_diff = halloc("oh_diff", [64, 2304], F32, 0)
        oh_dil = halloc("oh_dil", [64, 1152], F32, 9216)
        Fsb = halloc("Fsb", [20, 2304], F32, 13824)
        Fsb_dil = halloc("Fsb_dil", [20, 1152], F32, 23040)
        Hk = halloc("Hk", [128, 2176], F32, 27648)
        Eo = halloc("Eo", [128, 2176], BF16, 36352)
        lamb = halloc("lamb", [128, NL * 256], F32, 40704)
        lamt = halloc("lamt", [128, 64], F32, 44800)
        dots = halloc("dots", [128, 2 * NL], F32, 54912)
        mst0 = halloc("mst0", [128, 8, NMEM], F32, 45056)
        msq = halloc("msq", [128, NMEM], F32, 53248)
        tab_aug = halloc("tab_aug", [64, 20], F32, 54272)
        antiI = halloc("antiI", [128, 128], F32, 54272 + 128)

        gen = Ring([Buf(es.enter_context(nc.psum_tensor("gen%d" % i, [128, 512], F32)), "gen%d" % i)
                    for i in range(4)])
        acc4 = es.enter_context(nc.psum_tensor("acc4", [128, 4, 512], F32))
        accs = [Buf(acc4[:, i, :], "acc%d" % i) for i in range(4)]
        accring = Ring(accs)
        ptring = Ring(pts)
        ptring4 = Ring(pts[:4])

        RxT = [[Res("xT%d_%d" % (c, b)) for b in range(4)] for c in range(8)]
        RhT = [Res("hT%d" % b) for b in range(4)]
        RuT = [[Res("uT%d_%d" % (c, b)) for b in range(4)] for c in range(8)]
        Rq = [Res("q%d" % b) for b in range(4)]
        Rk = [Res("k%d" % b) for b in range(4)]
        Rq2 = [Res("q2_%d" % b) for b in range(4)]
        Rg = [Res("g%d" % b) for b in range(4)]
        Rv = [Res("v%d" % b) for b in range(4)]
        Rmerged = [[Res("mg%d_%d" % (c, b)) for b in range(4)] for c in range(8)]
        Rconst = Res("const")
        Rmisc = Res("misc")
        RaccSB = Res("accSB")
        Rmemn = Res("memn")
        Rmemst = Res("memstage")
        RFd = Res("Fd")
        REd = Res("Ed")
        Routs = [Res("out%d" % i) for i in range(4)]
        d_const = S.dma_sem("d_const")
        d_x = S.dma_sem("d_x")
        d_w = [S.dma_sem("d_w%d" % i) for i in range(3)]
        d_e = [S.dma_sem("d_e%d" % i) for i in range(2)]
        d_fd = S.dma_sem("d_fd")
        d_ident = S.dma_sem("d_ident")
        d_hk = S.dma_sem("d_hk")
        d_ed = S.dma_sem("d_ed")
        d_mem = S.dma_sem("d_mem")
        d_outs = [S.dma_sem("d_out%d" % i) for i in range(4)]

        blk = lambda b: slice(b * 512, (b + 1) * 512)

        def MM(out, lhsT, rhs, start, stop, reads, writes):
            return S.op("pe", lambda e: e.matmul(out, lhsT, rhs, start=start, stop=stop), reads, writes)

        def ACT(out, in_, func, reads, writes, bias=None, scale=1.0):
            if bias is None:
                return S.op("act", lambda e: e.activation(out, in_, func, scale=scale), reads, writes)
            return S.op("act", lambda e: e.activation(out, in_, func, bias=bias, scale=scale), reads, writes)

        def TT(out, in0, in1, op, reads, writes, eng="dve"):
            return S.op(eng, lambda e: e.tensor_tensor(out, in0, in1, op), reads, writes)

        def STT(out, in0, scalar, in1, op0, op1, reads, writes, eng="dve"):
            return S.op(eng, lambda e: e.scalar_tensor_tensor(out, in0, scalar, in1, op0, op1), reads, writes)

        def TCOPY(out, in_, reads, writes, eng="dve"):
            return S.op(eng, lambda e: e.tensor_copy(out, in_), reads, writes)

        def RECIP(out, in_, reads, writes):
            return S.op("dve", lambda e: e.reciprocal(out, in_), reads, writes)

        def MEMSET(ap, val, writes, eng="dve"):
            return S.op(eng, lambda e: e.memset(ap, val), (), writes)

        def DMA(eng, out, in_, dsem, reads, writes):
            return S.dma(eng, lambda e: e.dma_start(out=out, in_=in_), dsem, reads, writes)

        wcount = [0]

        def wload(l, u):
            sl = wslots[wcount[0] % 3]
            ds = d_w[wcount[0] % 3]
            wcount[0] += 1
            DMA("pool", sl.t[:, :], wts_d[l * NU + u], ds, [], [sl.r])
            return sl

        def w8(sl):
            return sl.t[:, :].rearrange("p (c n) -> p c n", c=8)

        def w4(sl):
            return sl.t[:, :].rearrange("p (c n) -> p c n", c=4)

        def rstd_from_bank(bank, ncols, dst):
            ACT(dst.t[:, 0:ncols], bank.t[:, 0:ncols], AF.Ln, [bank.r, Rmisc], [dst.r], bias=epsb[:, 0:1])
            ACT(dst.t[:, 0:ncols], dst.t[:, 0:ncols], AF.Exp, [dst.r], [dst.r], scale=-0.5)

        def setup():
            DMA("sp", gv[:, :], gv_d, d_const, [], [Rconst])
            DMA("sp", lamb[:, :], lam_d.partition_broadcast(128), d_const, [], [Rconst])
            DMA("sp", antiI[:, :], anti_d, d_const, [], [Rconst])
            DMA("pool", ident_bf[:, :], ident_d, d_ident, [], [Rconst])
            DMA("sp", oh_diff[:, :], ohd_d, d_const, [], [Rconst])
            DMA("sp", oh_dil[:, :], ohl_d, d_const, [], [Rconst])
            DMA("sp", mst0[:, :, :], memT_d, d_const, [], [Rconst])
            for c in range(8):
                DMA("sp", xT[:, c, :], xT_d[:, c, :], d_x, [], RxT[c])
            Rtab = Res("tab")
            MEMSET(tab_aug[0:64, :], 0.0, [Rtab])
            MEMSET(tab_aug[32:33, :], MASKNEG, [Rtab])
            DMA("sp", tab_aug[0:32, :], relb_d, d_const, [], [Rtab])
            MEMSET(ones_bf[:, :], 1.0, [Rmisc])
            MEMSET(ones_fD[:, :], 1.0 / 1024, [Rmisc])
            MEMSET(ones_fH[:, :], 1.0 / 128, [Rmisc])
            MEMSET(epsb[:, :], EPS, [Rmisc])
            S.fence()
            for n0 in range(0, 2304, 512):
                n = min(512, 2304 - n0)
                b = gen.next()
                MM(b.t[0:20, 0:n], tab_aug[0:64, 0:20], oh_diff[0:64, n0:n0 + n], True, True, [Rconst, Rmisc], [b.r])
                TCOPY(Fsb[0:20, n0:n0 + n], b.t[0:20, 0:n], [b.r], [Rmisc])
            for n0 in range(0, 1152, 384):
                b = gen.next()
                MM(b.t[0:20, 0:384], tab_aug[0:64, 0:20], oh_dil[0:64, n0:n0 + 384], True, True, [Rconst, Rmisc],
                   [b.r])
                TCOPY(Fsb_dil[0:20, n0:n0 + 384], b.t[0:20, 0:384], [b.r], [Rmisc])
            DMA("sp", Fd_diff.ap(), Fsb[:, :], d_fd, [Rmisc], [RFd])
            DMA("sp", Fd_dil.ap(), Fsb_dil[:, :], d_fd, [Rmisc], [RFd])
            RH = Res("Hk")
            REo = Res("Eo")
            Rbf = Res("biasfar")
            Rdm = Res("dilmask")
            for h in range(8):
                src = bass.AP(Fd_diff, h * 2304, [[1, 128], [1, 2176]])
                DMA("sp", Hk[:, :], src, d_hk, [RFd], [RH])
                for n0 in range(0, 2176, 512):
                    n = min(512, 2176 - n0)
                    b = gen.next()
                    MM(b.t[:, 0:n], antiI[:, :], Hk[:, n0:n0 + n], True, True, [RH, Rconst], [b.r])
                    ACT(Eo[:, n0:n0 + n], b.t[:, 0:n], AF.Exp, [b.r], [REo])
                    if n0 == 0:
                        TCOPY(biasfar[:, h, 1:2], b.t[:, 0:1], [b.r], [Rbf])
                    if n0 + n == 2176:
                        TCOPY(biasfar[:, h, 0:1], b.t[:, n - 1:n], [b.r], [Rbf])
                DMA("sp", Ed[h], Eo[:, :], d_ed, [REo], [REd])
            for g in range(3):
                for h in range(4):
                    gi = g * 4 + h
                    src = bass.AP(Fd_dil, (8 + gi) * 1152 + g * 384, [[1, 128], [1, 256]])
                    DMA("sp", Hk[:, 0:256], src, d_hk, [RFd], [RH])
                    b = gen.next()
                    MM(b.t[:, 0:256], antiI[:, :], Hk[:, 0:256], True, True, [RH, Rconst], [b.r])
                    if g == 2:
                        ACT(dilmask[:, gi, 0:128], b.t[:, 64:192], AF.Exp, [b.r], [Rdm])
                    else:
                        ACT(dilmask[:, gi, 0:128], b.t[:, 128:256], AF.Exp, [b.r], [Rdm])
                        ACT(dilmask[:, gi, 128:256], b.t[:, 0:128], AF.Exp, [b.r], [Rdm])
            b = gen.next()
            Rmsq = Res("msq")
            for c in range(8):
                ACT(msq[:, :], mst0[:, c, :], AF.Square, [Rconst], [Rmsq])
                MM(b.t[:, 0:NMEM], ones_fD[:, :], msq[:, :], c == 0, c == 7, [Rmsq, Rmisc], [b.r])
            rstd_from_bank(b, NMEM, Buf(rstd_mem, "rstd_mem"))
            Rl = Res("lam")
            for l in range(NL):
                for j in range(2):
                    o = l * 256 + j * 128
                    TT(lamt[:, :], lamb[:, o:o + 64], lamb[:, o + 64:o + 128], ALU.mult, [Rconst], [Rl])
                    dst = dots[:, 2 * l + j:2 * l + j + 1]
                    S.op("dve", lambda e, dst=dst: e.reduce_sum(dst, lamt[:, :], axis=AX.X), [Rl], [Rl])
            ACT(dots[:, :], dots[:, :], AF.Exp, [Rl], [Rl])
            for l in range(NL):
                lam_init = 0.8 - 0.6 * math.exp(-0.3 * l)
                TT(neglam[:, l:l + 1], dots[:, 2 * l + 1:2 * l + 2], dots[:, 2 * l:2 * l + 1], ALU.subtract,
                   [Rl], [Rl])
                nl_ = neglam[:, l:l + 1]
                S.op("dve", lambda e, nl_=nl_, li=lam_init: e.tensor_scalar_add(nl_, nl_, -li), [Rl], [Rl])
            S.fence()

        def ln_rstd(b_):
            bank = gen.next()
            for c in range(8):
                t = tmps[c % 2]
                ACT(t.t[:, :], xT[:, c, blk(b_)], AF.Square, [RxT[c][b_]], [t.r])
                MM(bank.t[:, :], ones_fD[:, :], t.t[:, :], c == 0, c == 7, [t.r], [bank.r])
            rs = tmps[2 + (b_ % 2)]
            rstd_from_bank(bank, 512, rs)
            return rs

        def layer_norm(l):
            for b_ in range(4):
                rs = ln_rstd(b_)
                for c in range(8):
                    STT(hT[:, c, blk(b_)], xT[:, c, blk(b_)], gv[:, l * 8 + c:l * 8 + c + 1], rs.t[:, :],
                        ALU.mult, ALU.mult, [RxT[c][b_], rs.r], [RhT[b_]])

        def final_norm():
            for b_ in range(4):
                rs = ln_rstd(b_)
                for c in range(8):
                    si = (b_ * 8 + c) % 4
                    st = ostage[si]
                    STT(st.t[:, :], xT[:, c, blk(b_)], gv[:, 64 + c:64 + c + 1], rs.t[:, :], ALU.mult, ALU.mult,
                        [RxT[c][b_], rs.r], [st.r])
                    DMA("sp", out_d[:, c, blk(b_)], st.t[:, :], d_outs[si], [st.r], [Routs[si]])
            S.wait_all("sp", [("d", d, d["count"]) for d in d_outs])

        bg_i = [0]

        def next_group():
            g = gen.items if bg_i[0] % 2 == 0 else accs
            bg_i[0] += 1
            return g

        def recip_act(dst_ap, src_ap, reads, writes):
            ACT(dst_ap, src_ap, AF.Ln, reads, writes)
            ACT(dst_ap, dst_ap, AF.Exp, writes, writes, scale=-1.0)

        def proj_fm(sl, col0, dst, dstR, kind, dst2=None, dst2R=None):
            W = w8(sl)
            banks = next_group()
            for c in range(8):
                for b_ in range(4):
                    MM(banks[b_].t[:, :], W[:, c, col0:col0 + 128], hT[:, c, blk(b_)], c == 0, c == 7,
                       [sl.r, RhT[b_]], [banks[b_].r])
            for b_ in range(4):
                bank = banks[b_]
                if kind == "silu":
                    ACT(dst[:, blk(b_)], bank.t[:, :], AF.Silu, [bank.r], [dstR[b_]])
                elif kind == "copy_act":
                    ACT(dst[:, blk(b_)], bank.t[:, :], AF.Copy, [bank.r], [dstR[b_]])
                elif kind == "qpad":
                    ACT(dst[0:64, blk(b_)], bank.t[0:64, :], AF.Copy, [bank.r], [dstR[b_]])
                    TCOPY(dst2[64:128, blk(b_)], bank.t[64:128, :], [bank.r], [dst2R[b_]])
                else:
                    TCOPY(dst[:, blk(b_)], bank.t[:, :], [bank.r], [dstR[b_]])

        def proj_v(sl, col0, tok_slices, dst=None, dstR=None, silu=False):
            W = w8(sl)
            if dst is None:
                dst, dstR = vT, Rv
            for t4 in range(4):
                bank = gen.next()
                for tt in range(4):
                    ts_ = tok_slices[t4 * 4 + tt]
                    for c in range(8):
                        MM(bank.t[:, tt * 128:(tt + 1) * 128], hT[:, c, ts_], W[:, c, col0:col0 + 128],
                           c == 0, c == 7, [sl.r] + RhT, [bank.r])
                src = bank.t[:, :].rearrange("p (t n) -> p t n", t=4)
                if silu:
                    ACT(dst[:, t4 * 4:(t4 + 1) * 4, 0:128], src, AF.Silu, [bank.r], [dstR[t4]])
                else:
                    TCOPY(dst[:, t4 * 4:(t4 + 1) * 4, 0:128], src, [bank.r], [dstR[t4]])

        def attn_run(tiles, scale):
            LAG = 2
            pend = []

            def flush_one():
                t, pt = pend.pop(0)
                nq = t["nq"]
                nb, na = t["num"]
                db, da = t["den"]
                if "k2" in t:
                    MM(na, t["v"], pt.t[:, 0:nq], True, False, [pt.r] + t["vR"], [nb.r])
                    MM(na, t["v2"], pt.t[:, nq:2 * nq], False, True, [pt.r] + t["vR"], [nb.r])
                    MM(da, ones_bf[:, :], pt.t[:, 0:nq], True, False, [pt.r], [db.r])
                    MM(da, ones_bf[:, :], pt.t[:, nq:2 * nq], False, True, [pt.r], [db.r])
                    t["post"]()
                    return
                MM(na, t["v"], pt.t[:, 0:nq], t["first"], t["last"], [pt.r] + t["vR"], [nb.r])
                seg = t.get("seg")
                if seg is None:
                    MM(da, ones_bf[:, :], pt.t[:, 0:nq], t["first"], t["last"], [pt.r], [db.r])
                else:
                    seg.append(pt)
                    if t["last"]:
                        for k_, p_ in enumerate(seg):
                            MM(da, ones_bf[:, :], p_.t[:, 0:nq], k_ == 0, k_ == len(seg) - 1, [p_.r], [db.r])
                if t.get("post") is not None:
                    t["post"]()

            for t in tiles:
                st = gen.next()
                nq = t["nq"]
                MM(st.t[:, 0:nq], t["k"], t["q"], True, True, t["kqR"], [st.r])
                if "k2" in t:
                    MM(st.t[:, nq:2 * nq], t["k2"], t["q"], True, True, t["kqR"], [st.r])
                    nq = 2 * nq
                pt = ptring4.next()
                if t["mode"] == "const":
                    ACT(pt.t[:, 0:nq], st.t[:, 0:nq], AF.Exp, [st.r], [pt.r], bias=t["bias"], scale=scale)
                else:
                    ACT(pt.t[:, 0:nq], st.t[:, 0:nq], AF.Exp, [st.r], [pt.r], scale=scale)
                    if t["mode"] == "mul":
                        TT(pt.t[:, 0:nq], pt.t[:, 0:nq], t["bias"], ALU.mult, [pt.r] + t["biasR"], [pt.r])
                pend.append((t, pt))
                if len(pend) > LAG:
                    flush_one()
            while pend:
                flush_one()

        def merge_round(l, ubase, nch):
            if nch == 8:
                gate_u = ubase + 2
            else:
                wbr_single = wload(l, ubase)
                gate_u = ubase + 1
            out_u = gate_u + 2
            for ecg in range(2):
                if nch == 8:
                    wb = wload(l, ubase + ecg)
                    Wb = w8(wb)
                    cbase = 0
                else:
                    wb = wbr_single
                    Wb = w4(wb)
                    cbase = ecg * 512
                gs = wload(l, gate_u + ecg)
                G = w8(gs)
                for ecl in range(4):
                    ec = ecg * 4 + ecl
                    cs = slice(cbase + ecl * 128, cbase + (ecl + 1) * 128)
                    for half in range(2):
                        bl = (2 * half, 2 * half + 1)
                        yb = [gen.next(), gen.next()]
                        for c in range(nch):
                            for k_, b_ in enumerate(bl):
                                MM(yb[k_].t[:, :], Wb[:, c, cs], uT[:, c, blk(b_)], c == 0, c == nch - 1,
                                   [wb.r, RuT[c][b_]], [yb[k_].r])
                        gb = [accring.next(), accring.next()]
                        for c in range(8):
                            for k_, b_ in enumerate(bl):
                                MM(gb[k_].t[:, :], G[:, c, ecl * 128:(ecl + 1) * 128], hT[:, c, blk(b_)], c == 0,
                                   c == 7, [gs.r, RhT[b_]], [gb[k_].r])
                        for k_, b_ in enumerate(bl):
                            gsb = gsig[k_]
                            ACT(gsb.t[:, :], gb[k_].t[:, :], AF.Sigmoid, [gb[k_].r], [gsb.r])
                            TT(merged[:, ec, blk(b_)], yb[k_].t[:, :], gsb.t[:, :], ALU.mult, [yb[k_].r, gsb.r],
                               [Rmerged[ec][b_]])
            for ocg in range(2):
                ws = wload(l, out_u + ocg)
                Wo = w8(ws)
                for ocl in range(4):
                    oc = ocg * 4 + ocl
                    obs = next_group()
                    for c in range(8):
                        for b_ in range(4):
                            MM(obs[b_].t[:, :], Wo[:, c, ocl * 128:(ocl + 1) * 128], merged[:, c, blk(b_)], c == 0,
                               c == 7, [ws.r, Rmerged[c][b_]], [obs[b_].r])
                    for b_ in range(4):
                        TT(xT[:, oc, blk(b_)], obs[b_].t[:, :], xT[:, oc, blk(b_)], ALU.add,
                           [obs[b_].r, RxT[oc][b_]], [RxT[oc][b_]])
            S.fence(("act", "dve"))

        def merge_round_MB(l, ubase):
            mtmp = ostage[0]
            for ecg in range(2):
                for part in range(2):
                    wb = wload(l, ubase + 4 * ecg + 2 * part)
                    gs = wload(l, ubase + 4 * ecg + 2 * part + 1)
                    Wb = w4(wb)
                    G = w8(gs)
                    c0 = 4 * part
                    for ecl in range(4):
                        ec = ecg * 4 + ecl
                        cs = slice(ecg * 512 + ecl * 128, ecg * 512 + (ecl + 1) * 128)
                        for half in range(2):
                            bl = (2 * half, 2 * half + 1)
                            yb = [gen.next(), gen.next()]
                            for c in range(4):
                                for k_, b_ in enumerate(bl):
                                    MM(yb[k_].t[:, :], Wb[:, c, cs], uT[:, c0 + c, blk(b_)], c == 0, c == 3,
                                       [wb.r, RuT[c0 + c][b_]], [yb[k_].r])
                            gb = [accring.next(), accring.next()]
                            for c in range(8):
                                for k_, b_ in enumerate(bl):
                                    MM(gb[k_].t[:, :], G[:, c, ecl * 128:(ecl + 1) * 128], hT[:, c, blk(b_)], c == 0,
                                       c == 7, [gs.r, RhT[b_]], [gb[k_].r])
                            for k_, b_ in enumerate(bl):
                                gsb = gsig[k_]
                                ACT(gsb.t[:, :], gb[k_].t[:, :], AF.Sigmoid, [gb[k_].r], [gsb.r])
                                if part == 0:
                                    TT(merged[:, ec, blk(b_)], yb[k_].t[:, :], gsb.t[:, :], ALU.mult,
                                       [yb[k_].r, gsb.r], [Rmerged[ec][b_]])
                                else:
                                    TT(mtmp.t[:, :], yb[k_].t[:, :], gsb.t[:, :], ALU.mult, [yb[k_].r, gsb.r],
                                       [mtmp.r])
                                    TT(merged[:, ec, blk(b_)], merged[:, ec, blk(b_)], mtmp.t[:, :], ALU.add,
                                       [Rmerged[ec][b_], mtmp.r], [Rmerged[ec][b_]])
            for ocg in range(2):
                ws = wload(l, ubase + 8 + ocg)
                Wo = w8(ws)
                for ocl in range(4):
                    oc = ocg * 4 + ocl
                    obs = next_group()
                    for c in range(8):
                        for b_ in range(4):
                            MM(obs[b_].t[:, :], Wo[:, c, ocl * 128:(ocl + 1) * 128], merged[:, c, blk(b_)], c == 0,
                               c == 7, [ws.r, Rmerged[c][b_]], [obs[b_].r])
                    for b_ in range(4):
                        TT(xT[:, oc, blk(b_)], obs[b_].t[:, :], xT[:, oc, blk(b_)], ALU.add,
                           [obs[b_].r, RxT[oc][b_]], [RxT[oc][b_]])
            S.fence(("act", "dve", "sp"))

        def branch_M(l):
            DMA("sp", memstage[:, :, :], memT_d, d_mem, [], [Rmemst])
            for c in range(8):
                STT(memnT[:, c, :], memstage[:, c, :], gv[:, 32 + l * 8 + c:32 + l * 8 + c + 1], rstd_mem[:, :],
                    ALU.mult, ALU.mult, [Rmemst], [Rmemn])
            scale = 128 ** -0.5
            for h in range(4):
                sl = wload(l, h)
                W = w8(sl)
                proj_fm(sl, 0, qT, Rq, "copy_act")
                proj_fm(sl, 128, gT, Rg, "silu")
                bank = gen.next()
                for c in range(8):
                    MM(bank.t[:, 0:NMEM], W[:, c, 256:384], memnT[:, c, :], c == 0, c == 7, [sl.r, Rmemn], [bank.r])
                TCOPY(kT[:, 0:NMEM], bank.t[:, 0:NMEM], [bank.r], [Rk[0]])
                bank = gen.next()
                for tt in range(2):
                    for c in range(8):
                        MM(bank.t[:, tt * 128:(tt + 1) * 128], memnT[:, c, tt * 128:(tt + 1) * 128],
                           W[:, c, 384:512], c == 0, c == 7, [sl.r, Rmemn], [bank.r])
                TCOPY(vT[:, 0:2, :], bank.t[:, 0:256].rearrange("p (t n) -> p t n", t=2), [bank.r], [Rv[0]])
                tiles = []
                for J in range(4):
                    nb = accs[(J % 2) * 2]
                    db = accs[(J % 2) * 2 + 1]
                    r1, t1 = tmps[J % 2], tmps[2 + J % 2]

                    def post(J=J, nb=nb, db=db, r1=r1, t1=t1, h=h):
                        recip_act(r1.t[:, :], db.t[:, :], [db.r], [r1.r])
                        TT(t1.t[:, :], nb.t[:, :], r1.t[:, :], ALU.mult, [nb.r, r1.r], [t1.r])
                        TT(uT[:, h, blk(J)], t1.t[:, :], gT[:, blk(J)], ALU.mult, [t1.r, Rg[J]], [RuT[h][J]])

                    for i in range(2):
                        tiles.append(dict(k=kT[:, i * 128:(i + 1) * 128], q=qT[:, blk(J)], nq=512, mode="none",
                                          kqR=[Rk[0], Rq[J]], v=vT[:, i, :], vR=[Rv[0]],
                                          num=(nb, nb.t[:, :]), den=(db, db.t[:, :]), first=(i == 0), last=(i == 1),
                                          post=(post if i == 1 else None)))
                attn_run(tiles, scale)
            S.fence(("act", "dve"))

        def branch_B(l):
            scale = 128 ** -0.5
            for h in range(4):
                for g, (_, r) in enumerate(DIL):
                    L = S_LEN // r
                    nseg = L // 128
                    sl = wload(l, 4 + h * 3 + g)
                    proj_fm(sl, 0, qT, Rq, "copy_act")
                    proj_fm(sl, 128, kT, Rk, "copy_dve")
                    if g == 0:
                        proj_fm(sl, 384, gT, Rg, "silu")
                    toks = []
                    for c in range(r):
                        for j in range(nseg):
                            s0 = c + r * 128 * j
                            toks.append(slice(s0, s0 + r * 127 + 1, r))
                    proj_v(sl, 256, toks)
                    gi = g * 4 + h
                    tiles = []
                    for c in range(r):
                        for s in (range(nseg + 1) if g < 2 else (0,)):
                            if g < 2:
                                qlo = max(0, 128 * s - 64)
                                qhi = min(L, 128 * s + 64)
                                js = [j for j in (s - 1, s) if 0 <= j < nseg]
                            else:
                                qlo, qhi, js = 0, L, [0]
                            nq = qhi - qlo
                            o = qlo - (128 * s - 64)
                            ab = accring.next()
                            qsl = slice(c + r * qlo, c + r * (qhi - 1) + 1, r)
                            seg = []

                            def post(ab=ab, nq=nq, qsl=qsl, g=g):
                                src = ab.t[:, :].rearrange("p (a n) -> p a n", a=2)[:, :, 0:nq]
                                dst = accSB[:, :, qsl]
                                if g == 0:
                                    ACT(dst, src, AF.Copy, [ab.r], [RaccSB])
                                else:
                                    TT(dst, src, dst, ALU.add, [ab.r, RaccSB], [RaccSB])

                            if g < 2 and len(js) == 2 and nq == 128 and o == 0:
                                ks2 = [slice(c + r * 128 * j, c + r * 128 * j + r * 127 + 1, r) for j in js]
                                tiles.append(dict(k=kT[:, ks2[0]], k2=kT[:, ks2[1]], q=qT[:, qsl], nq=nq, mode="mul",
                                                  bias=dilmask[:, gi, 0:256], biasR=[], kqR=Rk + Rq,
                                                  v=vT[:, c * nseg + js[0], :], v2=vT[:, c * nseg + js[1], :], vR=Rv,
                                                  num=(ab, ab.t[:, 0:nq]), den=(ab, ab.t[:, 256:256 + nq]),
                                                  first=True, last=True, post=post))
                                continue
                            for idx, j in enumerate(js):
                                ksl = slice(c + r * 128 * j, c + r * 128 * j + r * 127 + 1, r)
                                moff = ((0 if j == s - 1 else 128) + o) if g < 2 else 0
                                tiles.append(dict(k=kT[:, ksl], q=qT[:, qsl], nq=nq, mode="mul",
                                                  bias=dilmask[:, gi, moff:moff + nq], biasR=[],
                                                  kqR=Rk + Rq, v=vT[:, c * nseg + j, :], vR=Rv,
                                                  num=(ab, ab.t[:, 0:nq]), den=(ab, ab.t[:, 256:256 + nq]),
                                                  first=(idx == 0), last=(idx == len(js) - 1), seg=seg,
                                                  post=(post if idx == len(js) - 1 else None)))
                    attn_run(tiles, scale)
                recip_act(accSB[:, 1, :], accSB[:, 1, :], [RaccSB], [RaccSB])
                TT(accSB[:, 0, :], accSB[:, 0, :], accSB[:, 1, :], ALU.mult, [RaccSB], [RaccSB])
                TT(uT[:, 4 + h, :], accSB[:, 0, :], gT[:, :], ALU.mult, [RaccSB] + Rg, RuT[4 + h])
            S.fence(("act", "dve"))
            merge_round_MB(l, 16)

        def branch_A(l):
            scale = 64 ** -0.5
            cnorm = 1.0 - (0.8 - 0.6 * math.exp(-0.3 * l))
            MEMSET(qT[64:128, :], 0.0, Rq)
            MEMSET(q2T[0:64, :], 0.0, Rq2)
            MEMSET(vA[:, :, 128:129], 1.0, Rv)
            Rsm = Res("small")
            nat = [slice(t * 128, (t + 1) * 128) for t in range(16)]
            cn1, cn2 = tmps[1], tmps[2]
            c13 = cn1.t[:, :].rearrange("p (c e) -> p c e", c=4)
            c23 = cn2.t[:, :].rearrange("p (c e) -> p c e", c=4)
            accR = [a.r for a in accs]
            fin2 = []

            def pop_stage():
                if fin2:
                    fin2.pop(0)()

            def make_stages(h, J):
                bc = lambda ap: ap.unsqueeze(2).to_broadcast([128, 4, 128])

                def a1():
                    RECIP(dsm[:, :], dsm[:, :], [Rsm], [Rsm])

                def a2():
                    S.op("dve", lambda e: e.tensor_scalar_mul(dsm[:, 4:8], dsm[:, 4:8], neglam[:, l:l + 1]),
                         [Rsm], [Rsm])

                def a3():
                    TT(c13, c13, bc(dsm[:, 0:4]), ALU.mult, [cn1.r, Rsm], [cn1.r])

                def a4():
                    TT(c23, c23, bc(dsm[:, 4:8]), ALU.mult, [cn2.r, Rsm], [cn2.r])

                def a5():
                    TT(cn1.t[:, :], cn1.t[:, :], cn2.t[:, :], ALU.add, [cn1.r, cn2.r], [cn1.r])

                def b1():
                    TT(cn2.t[:, :], cn1.t[:, :], cn1.t[:, :], ALU.mult, [cn1.r], [cn2.r])

                def b2():
                    S.op("dve", lambda e: e.reduce_sum(rs4[:, :], c23, axis=AX.X), [cn2.r], [Rsm])

                def b3():
                    ACT(rs4[:, :], rs4[:, :], AF.Ln, [Rsm], [Rsm], bias=epsb[:, 0:1], scale=1.0 / 128)
                    ACT(rs4[:, :], rs4[:, :], AF.Exp, [Rsm], [Rsm], scale=-0.5)

                def c1():
                    S.op("dve", lambda e: e.tensor_scalar_mul(rs4[:, :], rs4[:, :], cnorm), [Rsm], [Rsm])

                def c2():
                    TT(c23, c13, bc(rs4[:, :]), ALU.mult, [cn1.r, Rsm], [cn2.r])

                def c3():
                    TT(utm.t[:, :, :], c23, gTM[:, 4 * J:4 * J + 4, :], ALU.mult, [cn2.r, Rg[J]], [utm.r])

                def c4():
                    tb = gen.next()
                    tbv = tb.t[:, :].bitcast(BF16)
                    for qc in range(4):
                        S.op("pe", lambda e, qc=qc: e.transpose(tbv[:, qc * 128:(qc + 1) * 128], utm.t[:, qc, :],
                                                                 ident_bf[:, :]), [utm.r], [tb.r])
                    ACT(uT[:, h, blk(J)], tbv[:, 0:512], AF.Copy, [tb.r], [RuT[h][J]])

                c3.reads_gate = True
                return [a1, a2, a3, a4, a5, b1, b2, b3, c1, c2, c3, c4]

            def pop_until_gate():
                while fin2 and any(getattr(f, "reads_gate", False) for f in fin2):
                    fin2.pop(0)()

            for h in range(8):
                sl = wload(l, 26 + h)
                es_ = estr[h % 2]
                DMA("sp", es_.t[:, :], Ed[h], d_e[h % 2], [REd], [es_.r])
                proj_fm(sl, 0, qT, Rq, "qpad", q2T, Rq2)
                for _ in range(5):
                    pop_stage()
                proj_fm(sl, 128, kT, Rk, "copy_dve")
                pop_until_gate()
                proj_v(sl, 384, nat, gTM, Rg, silu=True)
                while fin2:
                    fin2.pop(0)()
                proj_v(sl, 256, nat, vA, Rv)
                pend = []

                def flush_one():
                    J, m, i, pm = pend.pop(0)
                    for qc in range(4):
                        bank = accs[qc]
                        MM(bank.t[:, 0:129], pm.t[:, qc * 128:(qc + 1) * 128], vA[:, i, 0:129],
                           i == 0, i == 15, [pm.r, Rv[i // 4]], [bank.r])
                    if i == 15:
                        cn, c3 = (cn1, c13) if m == 0 else (cn2, c23)
                        if m == 0:
                            while fin2:
                                fin2.pop(0)()
                        TCOPY(c3, acc4[:, :, 0:128], accR, [cn.r])
                        TCOPY(dsm[:, 4 * m:4 * m + 4], acc4[:, :, 128], accR, [Rsm])
                        if m == 1:
                            fin2.extend(make_stages(h, J))
                    elif m == 0 and 1 <= i <= 13:
                        pop_stage()

                for J in range(4):
                    for m in range(2):
                        qsrc, Rqs = (qT, Rq) if m == 0 else (q2T, Rq2)
                        for i in range(16):
                            mm = i - 4 * J
                            near = -5 <= mm <= 8
                            st = gen.next()
                            MM(st.t[:, :], kT[:, i * 128:(i + 1) * 128], qsrc[:, blk(J)], True, True,
                               [Rk[i // 4], Rqs[J]], [st.r])
                            p = ptring.next()
                            if near:
                                col0 = 1024 - 128 * mm
                                ACT(p.t[:, :], st.t[:, :], AF.Exp, [st.r], [p.r], scale=scale)
                                TT(p.t[:, :], p.t[:, :], es_.t[:, col0:col0 + 512], ALU.mult, [p.r, es_.r], [p.r])
                            else:
                                fi = 1 if mm > 8 else 0
                                ACT(p.t[:, :], st.t[:, :], AF.Exp, [st.r], [p.r], bias=biasfar[:, h, fi:fi + 1],
                                    scale=scale)
                            pend.append((J, m, i, p))
                            if len(pend) > 3:
                                flush_one()
                while pend:
                    flush_one()
            while fin2:
                fin2.pop(0)()
            S.fence(("act", "dve"))
            merge_round(l, 34, 8)

        setup()
        for l in layers:
            layer_norm(l)
            S.fence()
            if "M" in branches:
                branch_M(l)
            if "B" in branches:
                branch_B(l)
            if "A" in branches:
                branch_A(l)
        if do_final:
            final_norm()
        else:
            for b_ in range(4):
                for c in range(8):
                    DMA("sp", out_d[:, c, blk(b_)], xT[:, c, blk(b_)], d_outs[0], [RxT[c][b_]], [Routs[0]])
            S.wait_all("sp", [("d", d_outs[0], d_outs[0]["count"])])
        S.emit()
        stats = dict(ninst=S.ninst, nwaits=S.nwaits, per_engine={e: len(S.ops[e]) for e in S.ENG})
    return nc, stats


def prep_inputs(x, mem, g_norm, w_in, diff_lambda, w_mem_kv, g_mem, w_br_diff, w_br_dil, w_br_mem, w_out,
                rel_bias, g_final):
    f = lambda a: np.asarray(a, dtype=np.float32)
    x, mem = f(x), f(mem)
    B = x.shape[0]
    wts = pack_weights(f(w_in), f(w_mem_kv), f(w_br_diff), f(w_br_dil), f(w_br_mem), f(w_out))
    gvec = np.zeros((128, 72), np.float32)
    for l in range(NL):
        gvec[:, l * 8:(l + 1) * 8] = f(g_norm)[l].reshape(8, 128).T
        gvec[:, 32 + l * 8:32 + (l + 1) * 8] = f(g_mem)[l].reshape(8, 128).T
    gvec[:, 64:72] = f(g_final).reshape(8, 128).T
    lam = np.ascontiguousarray(f(diff_lambda).reshape(1, NL * 256))
    oh_diff, oh_dil, anti = host_constants()
    in_maps = []
    for b in range(B):
        xTb = np.ascontiguousarray(x[b].reshape(S_LEN, 8, 128).transpose(2, 1, 0))
        mTb = np.ascontiguousarray(mem[b].reshape(NMEM, 8, 128).transpose(2, 1, 0))
        in_maps.append({"xT": xTb, "memT": mTb, "wts": wts, "gv": gvec, "lam": lam,
                        "relb": np.ascontiguousarray(f(rel_bias)), "ohdiff": oh_diff, "ohdil": oh_dil,
                        "anti": anti, "ident": np.eye(128, dtype=np.float32)})
    return in_maps


def unpack_out(res_list):
    outs = []
    for r in res_list:
        o = np.asarray(r["outT"], dtype=np.float32)
        outs.append(o.transpose(2, 1, 0).reshape(S_LEN, 1024))
    return np.stack(outs, axis=0)


def kernel(x, mem, g_norm, w_in, diff_lambda, w_mem_kv, g_mem, w_br_diff, w_br_dil, w_br_mem, w_out,
           rel_bias, g_final):
    in_maps = prep_inputs(x, mem, g_norm, w_in, diff_lambda, w_mem_kv, g_mem, w_br_diff, w_br_dil,
                          w_br_mem, w_out, rel_bias, g_final)
    nc, _ = build_program()
    res = run_bass_kernel_spmd(nc, in_maps, core_ids=list(range(len(in_maps))))
    return unpack_out(res.results)
```
